# Optimizing a Trainium2 kernel written in Bass

```python
import math
import jax
import jax.numpy as jnp
from jax import lax
import numpy as np

D_MODEL = 1024
BATCH = 4
SEQ = 8192
DEPTH = 1

GRID_W = 64
CTX_LEN = 256
NA_HEADS = 8
NA_HEAD_DIM = 64
NA_WIDTH = NA_HEADS * NA_HEAD_DIM
NA_KH = 8
NA_KW = 16
DN_HEADS = 8
DN_DK = 64
DN_DV = 64
DN_QK_W = DN_HEADS * DN_DK
DN_WIDTH = DN_HEADS * DN_DV
DN_CONV_W = 2 * DN_QK_W + DN_WIDTH
DN_CHUNK = 64
CONV_K = 3
ROPE_BASE = 10000.0
EPS = 1e-6
PROJ_WIDTH = 4 * NA_WIDTH + DN_CONV_W + DN_WIDTH + 4 * DN_HEADS + 2 * D_MODEL

kernel_name = 'hybrid_na_gdn_prefix_block'


def _rmsnorm(x, w):
    xf = x.astype(jnp.float32)
    y = xf * lax.rsqrt(jnp.mean(xf * xf, axis=-1, keepdims=True) + EPS)
    return (y * w.astype(jnp.float32)).astype(x.dtype)


def _l2norm(x):
    xf = x.astype(jnp.float32)
    return (xf * lax.rsqrt(jnp.sum(xf * xf, axis=-1, keepdims=True) + EPS)).astype(x.dtype)


def _heads(a, n_heads):
    return a.reshape(a.shape[:-1] + (n_heads, a.shape[-1] // n_heads))


def _rope_1d(x, pos):
    half = x.shape[-1] // 2
    freqs = ROPE_BASE ** (-jnp.arange(half, dtype=jnp.float32) / half)
    ang = pos.astype(jnp.float32)[:, None] * freqs[None, :]
    cos = jnp.concatenate([jnp.cos(ang), jnp.cos(ang)], -1)[None, :, None, :]
    sin = jnp.concatenate([jnp.sin(ang), jnp.sin(ang)], -1)[None, :, None, :]
    xf = x.astype(jnp.float32)
    rot = jnp.concatenate([-xf[..., half:], xf[..., :half]], -1)
    return (xf * cos + rot * sin).astype(x.dtype)


def _rope_2d(x, pos_row, pos_col):
    d = x.shape[-1] // 2
    return jnp.concatenate([_rope_1d(x[..., :d], pos_row), _rope_1d(x[..., d:], pos_col)], -1)


def _short_conv(x, w):
    return lax.conv_general_dilated(
        x, w.astype(x.dtype)[:, None, :], window_strides=(1,),
        padding=[(CONV_K // 2, CONV_K // 2)],
        dimension_numbers=('NWC', 'WIO', 'NWC'),
        feature_group_count=x.shape[-1])


def _split_proj(p):
    sizes = (NA_WIDTH, NA_WIDTH, NA_WIDTH, NA_WIDTH, DN_CONV_W, DN_WIDTH,
             2 * DN_HEADS, 2 * DN_HEADS, D_MODEL, D_MODEL)
    return jnp.split(p, np.cumsum(sizes)[:-1].tolist(), axis=-1)


def _na_latent(q, k, v, kc, vc, rpb):
    B, T, H, Dh = q.shape
    rows = T // GRID_W
    kh = min(NA_KH, rows)
    kg = k.reshape(B, rows, GRID_W, H, Dh)
    vg = v.reshape(B, rows, GRID_W, H, Dh)
    qg = q.reshape(B, rows, GRID_W, H, Dh)
    cq = jnp.arange(GRID_W)
    c0 = jnp.clip(cq - NA_KW // 2, 0, GRID_W - NA_KW)
    col_in = (cq[None, :] >= c0[:, None]) & (cq[None, :] < c0[:, None] + NA_KW)
    dx = jnp.clip(cq[None, :] - cq[:, None], -(NA_KW - 1), NA_KW - 1) + (NA_KW - 1)
    rpb_cols = rpb[:, :, dx]

    def row_step(args):
        r, q_r = args
        r0 = jnp.clip(r - kh // 2, 0, rows - kh)
        k_band = lax.dynamic_slice_in_dim(kg, r0, kh, axis=1)
        v_band = lax.dynamic_slice_in_dim(vg, r0, kh, axis=1)
        dy = r0 + jnp.arange(kh) - r + (NA_KH - 1)
        bias = jnp.take(rpb_cols, dy, axis=1).transpose(0, 2, 1, 3)
        s_win = jnp.einsum('bqhd,bywhd->bhqyw', q_r, k_band).astype(jnp.float32) + bias[None].astype(jnp.float32)
        s_win = jnp.where(col_in[:, None, :], s_win, -jnp.inf)
        s_ctx = jnp.einsum('bqhd,bchd->bhqc', q_r, kc).astype(jnp.float32)
        s = jnp.concatenate([s_win.reshape(B, H, GRID_W, kh * GRID_W), s_ctx], axis=-1)
        pr = jax.nn.softmax(s, axis=-1).astype(v.dtype)
        p_win = pr[..., :kh * GRID_W].reshape(B, H, GRID_W, kh, GRID_W)
        p_ctx = pr[..., kh * GRID_W:]
        return (jnp.einsum('bhqyw,bywhd->bqhd', p_win, v_band)
                + jnp.einsum('bhqc,bchd->bqhd', p_ctx, vc))

    o = lax.map(row_step, (jnp.arange(rows), jnp.moveaxis(qg, 1, 0)))
    return jnp.moveaxis(o, 0, 1).reshape(B, T, H, Dh)


def _dense_attention(q, k, v):
    s = jnp.einsum('bqhd,bkhd->bhqk', q, k).astype(jnp.float32)
    p = jax.nn.softmax(s, axis=-1).astype(v.dtype)
    return jnp.einsum('bhqk,bkhd->bqhd', p, v)


def _gdn_chunked(q, k, v, g, beta, s0):
    out_dtype = v.dtype
    f32 = jnp.float32
    B, T, H, DK = q.shape
    DV = v.shape[-1]
    C = DN_CHUNK
    N = T // C

    def blk(a):
        a = a.astype(f32).reshape((B, N, C, H) + a.shape[3:])
        return jnp.swapaxes(a, 2, 3)

    q, k, v, g, beta = blk(q), blk(k), blk(v), blk(g), blk(beta)
    gc = jnp.cumsum(g, axis=-1)
    incl = jnp.tril(jnp.ones((C, C), bool))
    strict = jnp.tril(jnp.ones((C, C), bool), -1)
    decay = jnp.exp(jnp.where(incl, gc[..., :, None] - gc[..., None, :], -jnp.inf))
    kb = k * beta[..., None]
    lower = jnp.where(strict, jnp.einsum('bnhid,bnhjd->bnhij', kb, k) * decay, 0.0)
    a_mat = lower + jnp.eye(C, dtype=f32)
    rhs = jnp.concatenate([v * beta[..., None], kb * jnp.exp(gc)[..., None]], axis=-1)
    sol = lax.linalg.triangular_solve(a_mat, rhs, left_side=True, lower=True, unit_diagonal=True)
    u, wk = sol[..., :DV], sol[..., DV:]
    qk = jnp.einsum('bnhid,bnhjd->bnhij', q, k) * decay
    q_dec = q * jnp.exp(gc)[..., None]
    k_dec = k * jnp.exp(gc[..., -1:] - gc)[..., None]
    g_tot = jnp.exp(gc[..., -1])

    def step(S, xs):
        qk_i, qd_i, kd_i, u_i, w_i, gt_i = xs
        v_new = u_i - jnp.einsum('bhck,bhkv->bhcv', w_i, S)
        o_i = jnp.einsum('bhck,bhkv->bhcv', qd_i, S) + jnp.einsum('bhij,bhjv->bhiv', qk_i, v_new)
        S = S * gt_i[..., None, None] + jnp.einsum('bhck,bhcv->bhkv', kd_i, v_new)
        return S, o_i

    xs = (jnp.moveaxis(qk, 1, 0), jnp.moveaxis(q_dec, 1, 0), jnp.moveaxis(k_dec, 1, 0),
          jnp.moveaxis(u, 1, 0), jnp.moveaxis(wk, 1, 0), jnp.moveaxis(g_tot, 1, 0))
    s_fin, o = lax.scan(step, s0.astype(f32), xs)
    o = jnp.transpose(o, (1, 0, 3, 2, 4)).reshape(B, T, H, DV)
    return o.astype(out_dtype), s_fin


def _dn_prepare(qkv_raw, b_raw, a_raw, w, pos):
    B, T, _ = qkv_raw.shape
    f32 = jnp.float32
    qkv = jax.nn.silu(_short_conv(qkv_raw, w['conv_w']))
    q, k, v = jnp.split(qkv, [DN_QK_W, 2 * DN_QK_W], axis=-1)
    q = _l2norm(_heads(q, DN_HEADS))
    k = _l2norm(_heads(k, DN_HEADS))
    v = _heads(v, DN_HEADS)
    if pos is not None:
        q = _rope_2d(q, pos[0], pos[1])
        k = _rope_2d(k, pos[0], pos[1])
    q = q * DN_DK ** -0.5
    beta = jax.nn.sigmoid(b_raw.astype(f32)).reshape(B, T, 2, DN_HEADS)
    g = -jnp.exp(w['dn_A_log'].astype(f32)) * jax.nn.softplus(
        a_raw.astype(f32).reshape(B, T, 2, DN_HEADS) + w['dn_dt_bias'].astype(f32))
    return q, k, v, beta, g


def _bidir_delta(q, k, v, beta, g, s0_f, s0_b):
    rev = lambda a: jnp.flip(a, axis=1)
    o_f, s_f = _gdn_chunked(q, k, v, g[:, :, 0], beta[:, :, 0], s0_f)
    o_b, s_b = _gdn_chunked(rev(q), rev(k), rev(v), rev(g[:, :, 1]), rev(beta[:, :, 1]), s0_b)
    return o_f + rev(o_b), s_f, s_b


def _merge(o_na, z_na, o_dn, z_dn, g_na, g_dn, w):
    B, T = o_na.shape[:2]
    u_na = (o_na.reshape(B, T, NA_WIDTH) * jax.nn.silu(z_na)) @ w['w_o_na']
    o_dn = _rmsnorm(o_dn, w['dn_norm_w']).reshape(B, T, DN_WIDTH)
    u_dn = (o_dn * jax.nn.silu(z_dn)) @ w['w_o_dn']
    y = jax.nn.sigmoid(g_na) * u_na + jax.nn.sigmoid(g_dn) * u_dn
    return y @ w['w_out']


def _layer(x, ctx, c, c_ctx, w, last):
    B, T, _ = x.shape
    t = jnp.arange(T)
    pos = (t // GRID_W, t % GRID_W)
    shift, scale, gate = jnp.split(jax.nn.silu(c) @ w['mod_w'] + w['mod_b'], 3, axis=-1)
    shift_c, scale_c, gate_c = jnp.split(jax.nn.silu(c_ctx) @ w['mod_w'] + w['mod_b'], 3, axis=-1)
    h = _rmsnorm(x, w['norm_w']) * (1.0 + scale[:, None]) + shift[:, None]
    hc = _rmsnorm(ctx, w['norm_w']) * (1.0 + scale_c) + shift_c
    na_q, na_k, na_v, na_z, dn_qkv, dn_z, dn_b, dn_a, g_na, g_dn = _split_proj(h @ w['w_in'])
    na_qc, na_kc, na_vc, na_zc, dn_qkvc, dn_zc, dn_bc, dn_ac, g_nac, g_dnc = _split_proj(hc @ w['w_in'])

    q = _rmsnorm(_heads(na_q, NA_HEADS), w['na_q_norm']) * NA_HEAD_DIM ** -0.5
    k = _rmsnorm(_heads(na_k, NA_HEADS), w['na_k_norm'])
    v = _heads(na_v, NA_HEADS)
    kc = _rmsnorm(_heads(na_kc, NA_HEADS), w['na_k_norm'])
    vc = _heads(na_vc, NA_HEADS)
    o_na = _na_latent(q, k, v, kc, vc, w['na_rpb'])

    dq, dk, dv, dbeta, dg = _dn_prepare(dn_qkv, dn_b, dn_a, w, pos)
    dqc, dkc, dvc, dbetac, dgc = _dn_prepare(dn_qkvc, dn_bc, dn_ac, w, None)
    zeros = jnp.zeros((ctx.shape[0], DN_HEADS, DN_DK, DN_DV), jnp.float32)
    o_dnc, s_f, s_b = _bidir_delta(dqc, dkc, dvc, dbetac, dgc, zeros, zeros)
    o_dn, _, _ = _bidir_delta(dq, dk, dv, dbeta, dg, s_f, s_b)

    x = x + gate[:, None] * _merge(o_na, na_z, o_dn, dn_z, g_na, g_dn, w)
    if not last:
        qc = _rmsnorm(_heads(na_qc, NA_HEADS), w['na_q_norm']) * NA_HEAD_DIM ** -0.5
        o_nac = _dense_attention(qc, kc, vc)
        ctx = ctx + gate_c * _merge(o_nac, na_zc, o_dnc, dn_zc, g_nac, g_dnc, w)
    return x, ctx


def setup_inputs(seed: int = 0) -> dict:
    key = jax.random.key(seed)
    ks = jax.random.split(key, 20)
    f32 = jnp.float32

    def nrm(k, shape, s):
        return jax.random.normal(k, shape, f32) * s

    x = nrm(ks[0], (BATCH, SEQ, D_MODEL), 1.0)
    c = nrm(ks[1], (BATCH, D_MODEL), 1.0)
    ctx = nrm(ks[2], (BATCH, CTX_LEN, D_MODEL), 1.0)
    c_ctx = nrm(ks[3], (D_MODEL,), 1.0)
    mod_w = nrm(ks[4], (DEPTH, D_MODEL, 3 * D_MODEL), 0.5 * D_MODEL ** -0.5)
    mod_b = nrm(ks[5], (DEPTH, 3 * D_MODEL), 0.01)
    norm_w = 1.0 + nrm(ks[6], (DEPTH, D_MODEL), 0.01)
    w_in = nrm(ks[7], (DEPTH, D_MODEL, PROJ_WIDTH), D_MODEL ** -0.5)
    conv_w = nrm(ks[8], (DEPTH, CONV_K, DN_CONV_W), CONV_K ** -0.5)
    na_q_norm = 1.0 + nrm(ks[9], (DEPTH, NA_HEAD_DIM), 0.01)
    na_k_norm = 1.0 + nrm(ks[10], (DEPTH, NA_HEAD_DIM), 0.01)
    na_rpb = nrm(ks[11], (DEPTH, NA_HEADS, 2 * NA_KH - 1, 2 * NA_KW - 1), 0.1)
    dn_A_log = jnp.log(jax.random.uniform(ks[12], (DEPTH, 2, DN_HEADS), f32, 1.0, 16.0))
    dt = jnp.exp(jax.random.uniform(ks[13], (DEPTH, 2, DN_HEADS), f32, math.log(1e-3), math.log(1e-1)))
    dn_dt_bias = dt + jnp.log(-jnp.expm1(-dt))
    dn_norm_w = 1.0 + nrm(ks[14], (DEPTH, DN_DV), 0.01)
    w_o_na = nrm(ks[15], (DEPTH, NA_WIDTH, D_MODEL), NA_WIDTH ** -0.5)
    w_o_dn = nrm(ks[16], (DEPTH, DN_WIDTH, D_MODEL), DN_WIDTH ** -0.5)
    w_out = nrm(ks[17], (DEPTH, D_MODEL, D_MODEL), D_MODEL ** -0.5)
    return {'x': x, 'c': c, 'ctx': ctx, 'c_ctx': c_ctx, 'mod_w': mod_w, 'mod_b': mod_b,
            'norm_w': norm_w, 'w_in': w_in, 'conv_w': conv_w, 'na_q_norm': na_q_norm,
            'na_k_norm': na_k_norm, 'na_rpb': na_rpb, 'dn_A_log': dn_A_log, 'dn_dt_bias': dn_dt_bias,
            'dn_norm_w': dn_norm_w, 'w_o_na': w_o_na, 'w_o_dn': w_o_dn, 'w_out': w_out}


def reference(x, c, ctx, c_ctx, mod_w, mod_b, norm_w, w_in, conv_w, na_q_norm, na_k_norm, na_rpb,
              dn_A_log, dn_dt_bias, dn_norm_w, w_o_na, w_o_dn, w_out):
    for l in range(DEPTH):
        w = {'mod_w': mod_w[l], 'mod_b': mod_b[l], 'norm_w': norm_w[l], 'w_in': w_in[l],
             'conv_w': conv_w[l], 'na_q_norm': na_q_norm[l], 'na_k_norm': na_k_norm[l],
             'na_rpb': na_rpb[l], 'dn_A_log': dn_A_log[l], 'dn_dt_bias': dn_dt_bias[l],
             'dn_norm_w': dn_norm_w[l], 'w_o_na': w_o_na[l], 'w_o_dn': w_o_dn[l], 'w_out': w_out[l]}
        x, ctx = _layer(x, ctx, c, c_ctx, w, l == DEPTH - 1)
    return x
```

```python
import contextlib
import math
import numpy as np
import concourse.bass as bass
import concourse.mybir as mybir
from concourse.bass_utils import run_bass_kernel_spmd

F32 = mybir.dt.float32
BF16 = mybir.dt.bfloat16
U32 = mybir.dt.uint32
ALU = mybir.AluOpType
AF = mybir.ActivationFunctionType
AX = mybir.AxisListType

T = 8192
D = 1024
OWN = 4096
CTX = 256
EPS = 1e-6
GDT = BF16
NEG = -30000.0


def _tname(ap):
    t = getattr(ap, "tensor", None)
    return None if t is None else t.name


class Prog:
    ENG = ["pe", "dve", "act", "pool", "sp"]

    def __init__(self, nc, pool):
        self.nc = nc
        self.pool = pool
        self.q = {e: [] for e in self.ENG}
        self.lastw = {}
        self.readers = {}
        self.waited = {e: {} for e in self.ENG}
        self.dcount = dict(pool["dcount"])

    def _deps(self, E, reads, writes):
        evs = []
        raw = set()
        for r in reads:
            if r in self.lastw:
                evs.append(self.lastw[r])
                raw.add(self.lastw[r])
        for w in writes:
            if w in self.lastw:
                evs.append(self.lastw[w])
            evs += self.readers.get(w, [])
        waits = {}
        for ev in evs:
            if ev[0] == "c":
                if ev[1] == E and (E == "pe" or ev not in raw):
                    continue
                key = ("c", ev[1])
            else:
                key = ("d", ev[1])
            val = ev[2]
            if self.waited[E].get(key, -1) >= val:
                continue
            if waits.get(key, -1) < val:
                waits[key] = val
        for k, v in waits.items():
            self.waited[E][k] = v
            if k[0] == "c":
                self.q[k[1]][v]["inc"] = True
        return waits

    def _commit(self, ev, reads, writes):
        for r in reads:
            self.readers.setdefault(r, []).append(ev)
        for w in writes:
            self.lastw[w] = ev
            self.readers[w] = []

    def op(self, E, fn, reads, writes):
        reads = [r for r in reads if r is not None]
        writes = [w for w in writes if w is not None]
        waits = self._deps(E, reads, writes)
        idx = len(self.q[E])
        self.q[E].append(dict(fn=fn, waits=waits, inc=False, dma=None))
        self._commit(("c", E, idx), reads, writes)

    def dma(self, out, in_, Q="sp", reads=None, writes=None):
        on, inn = _tname(out), _tname(in_)
        reads = [inn] if reads is None else reads
        writes = [on] if writes is None else writes
        sb_out = "SB" in out.tensor.__class__.__name__
        kn = on if sb_out else inn
        if kn[0] == "p" and "_" in kn and kn[1:kn.index("_")].isdigit():
            kn = kn[kn.index("_") + 1:]
        key = "%s:%s" % (Q, kn)
        waits = self._deps(Q, reads, writes)
        cnt = self.dcount.get(key, 0) + 16
        self.dcount[key] = cnt
        self.q[Q].append(dict(fn=lambda e: e.dma_start(out=out, in_=in_), waits=waits, inc=False, dma=key))
        self._commit(("d", key, cnt), reads, writes)

    def _rw(self, out, ins, extra_w=()):
        w = [_tname(out)] + [_tname(a) for a in extra_w]
        r = [_tname(a) for a in ins if hasattr(a, "tensor")]
        return r, w

    def mm(self, out, lhsT, rhs, start=True, stop=True):
        r, w = self._rw(out, [lhsT, rhs])
        self.op("pe", lambda e: e.matmul(out, lhsT, rhs, start=start, stop=stop), r, w)

    def tr(self, out, in_, ident):
        r, w = self._rw(out, [in_, ident])
        self.op("pe", lambda e: e.transpose(out, in_, ident), r, w)

    def tt(self, E, out, in0, in1, op):
        r, w = self._rw(out, [in0, in1])
        self.op(E, lambda e: e.tensor_tensor(out, in0, in1, op), r, w)

    def ts(self, E, out, in0, s1, s2, op0, op1=None):
        ins = [in0] + [s for s in (s1, s2) if hasattr(s, "tensor")]
        r, w = self._rw(out, ins)
        if op1 is None:
            self.op(E, lambda e: e.tensor_scalar(out, in0, s1, None, op0), r, w)
        else:
            self.op(E, lambda e: e.tensor_scalar(out, in0, s1, s2, op0, op1), r, w)

    def stt(self, out, in0, scalar, in1, op0, op1):
        ins = [in0, in1] + ([scalar] if hasattr(scalar, "tensor") else [])
        r, w = self._rw(out, ins)
        self.op("dve", lambda e: e.scalar_tensor_tensor(out, in0, scalar, in1, op0, op1), r, w)

    def cp(self, E, out, in_):
        r, w = self._rw(out, [in_])
        if E == "act":
            self.op(E, lambda e: e.copy(out, in_), r, w)
        else:
            self.op(E, lambda e: e.tensor_copy(out, in_), r, w)

    def actv(self, out, in_, func, bias=None, scale=None, accum_out=None):
        ins = [in_] + [s for s in (bias, scale) if hasattr(s, "tensor")]
        r, w = self._rw(out, ins, [accum_out] if accum_out is not None else [])
        kw = {}
        if bias is not None:
            kw["bias"] = bias
        if scale is not None:
            kw["scale"] = scale
        if accum_out is not None:
            kw["accum_out"] = accum_out
        self.op("act", lambda e: e.activation(out, in_, func, **kw), r, w)

    def recip(self, out, in_):
        r, w = self._rw(out, [in_])
        self.op("dve", lambda e: e.reciprocal(out, in_), r, w)

    def memset(self, E, ap, val):
        r, w = self._rw(ap, [])
        self.op(E, lambda e: e.memset(ap, val), r, w)

    def reduce(self, E, out, in_, op):
        r, w = self._rw(out, [in_])
        self.op(E, lambda e: e.tensor_reduce(out, in_, AX.X, op), r, w)

    def cpred(self, out, mask, data):
        r, w = self._rw(out, [mask, data, out])
        self.op("dve", lambda e: e.copy_predicated(out, mask, data), r, w)

    def emit(self):
        nc = self.nc
        pool = self.pool
        with contextlib.ExitStack() as st:
            pool["phase"] += 1
            if "csem" not in pool:
                pool["csem"] = {e: pool["stack"].enter_context(nc.semaphore("cs_%s" % e)) for e in self.ENG}
                pool["cbase"] = {e: 0 for e in self.ENG}
            csem = pool["csem"]
            cbase = dict(pool["cbase"])
            dsem = pool["dsem"]
            for k in sorted(self.dcount):
                if k not in dsem:
                    dsem[k] = pool["stack"].enter_context(nc.semaphore("ds%d" % len(dsem)))
            cum = {}
            for e in self.ENG:
                c = 0
                arr = []
                for rec in self.q[e]:
                    if rec["inc"] and rec["dma"] is None:
                        c += 1
                    arr.append(c)
                cum[e] = arr
            block = st.enter_context(nc.Block())
            engobj = {"pe": block.tensor, "dve": block.vector, "act": block.scalar,
                      "pool": block.gpsimd, "sp": block.sync}
            for e in self.ENG:
                recs = self.q[e]
                my_dma = {}
                for rec in recs:
                    if rec["dma"] is not None:
                        my_dma[rec["dma"]] = self.dcount[rec["dma"]]

                def body(eng, recs=recs, e=e, my_dma=my_dma):
                    for rec in recs:
                        for k, v in rec["waits"].items():
                            if k[0] == "c":
                                eng.wait_ge(csem[k[1]], cbase[k[1]] + cum[k[1]][v])
                            else:
                                eng.wait_ge(dsem[k[1]], v)
                        ins = rec["fn"](eng)
                        if rec["dma"] is not None:
                            ins.then_inc(dsem[rec["dma"]], 16)
                        elif rec["inc"]:
                            ins.then_inc(csem[e], 1)
                    for k, v in my_dma.items():
                        eng.wait_ge(dsem[k], v)
                engobj[e](body)
        for e in self.ENG:
            if cum[e]:
                pool["cbase"][e] += cum[e][-1]
        pool["dcount"] = dict(self.dcount)


CF = ["ident", "ones", "tri0", "tri1", "ms0", "ms1", "mi0", "mi1", "bd64", "rp", "oz0", "oz1"]
LEVELS = [1, 2, 4, 8, 16, 32, 64]


def host_consts():
    i = np.arange(128)
    p, f = i[:, None], i[None, :]
    c = {}
    c["ident"] = (p == f)
    c["ones"] = np.ones((128, 128), bool)
    c["tri0"] = (p <= f)
    c["tri1"] = (p >= f)
    c["ms0"] = (f < p)
    c["ms1"] = (f > p)
    c["mi0"] = (f <= p)
    c["mi1"] = (f >= p)
    c["bd64"] = (p // 64 == f // 64)
    partner = np.where((i % 32) < 16, i + 16, i - 16)
    c["rp"] = (p == partner[None, :])
    c["oz0"] = np.broadcast_to(f < 64, (128, 128))
    c["oz1"] = np.broadcast_to(f >= 64, (128, 128))
    cf = np.stack([c[k].astype(np.float32) for k in CF], axis=1)
    lm = np.stack([((p // (2 * b) == f // (2 * b)) & (p // b != f // b)).astype(np.uint32) for b in LEVELS], axis=1)
    return np.ascontiguousarray(cf), np.ascontiguousarray(lm)


def rope_tables(flip):
    rows = T // 64
    freqs = (np.float32(10000.0) ** (-np.arange(16, dtype=np.float32) / np.float32(16))).astype(np.float32)
    tab = np.zeros((128, 4, 128), np.float32)
    for p in range(128):
        d = p % 64
        dd = d % 32
        sgn = -1.0 if dd < 16 else 1.0
        f = freqs[dd % 16]
        if d < 32:
            for m in range(rows):
                orow = (rows - 1 - m) if flip else m
                ang = np.float32(orow) * f
                tab[m, 0, p] = np.cos(ang)
                tab[m, 2, p] = np.sin(ang) * sgn
        else:
            for m in range(64):
                ocol = (63 - m) if flip else m
                ang = np.float32(ocol) * f
                tab[m, 1, p] = np.cos(ang)
                tab[m, 3, p] = np.sin(ang) * sgn
    return tab


def na_bias_tables(rpb, flip):
    def orig_rc(view_tile):
        u = view_tile * 128 + np.arange(128)
        t = (T - 1 - u) if flip else u
        return t // 64, t % 64

    def tile(qt, kt):
        qr, qc = orig_rc(qt)
        kr, kc = orig_rc(kt)
        r0 = np.clip(qr - 4, 0, T // 64 - 8)
        c0 = np.clip(qc - 8, 0, 48)
        valid = ((kr[:, None] >= r0[None, :]) & (kr[:, None] < r0[None, :] + 8)
                 & (kc[:, None] >= c0[None, :]) & (kc[:, None] < c0[None, :] + 16))
        dy = np.clip(kr[:, None] - qr[None, :] + 7, 0, 14)
        dx = np.clip(kc[:, None] - qc[None, :], -15, 15) + 15
        g = rpb[:, dy, dx]
        return np.where(valid[None], g, np.float32(NEG)).astype(np.float32)

    tabs = []
    for off in range(-2, 3):
        tabs.append(tile(2, 2 + off))
    for qt in range(2):
        for kt in range(4):
            tabs.append(tile(qt, kt))
    out = np.concatenate(tabs, axis=0)
    return np.ascontiguousarray(out.transpose(1, 0, 2))


def fm(v, n=None):
    return np.ascontiguousarray(v.reshape(-1, 128).T)


def prep_core(inp, b, half, cf, lm):
    flip = half == 1
    x = inp["x"][b]
    ctx = inp["ctx"][b]
    w_in = inp["w_in"][0]
    m = {}
    m["xv"] = np.ascontiguousarray(x[::-1]) if flip else x
    m["ctxv"] = np.ascontiguousarray(ctx[::-1]) if flip else ctx
    m["cT"] = np.ascontiguousarray(np.stack([fm(inp["c"][b]), fm(inp["c_ctx"])], axis=2))
    m["mod_w"] = inp["mod_w"][0]
    m["mod_bT"] = fm(inp["mod_b"][0])
    m["modb_gate"] = np.ascontiguousarray(np.broadcast_to(inp["mod_b"][0][None, 2048:3072], (128, 1024)))
    m["norm_wT"] = fm(inp["norm_w"][0])
    m["wna"] = np.ascontiguousarray(w_in[:, 0:1536])
    m["wdn"] = np.ascontiguousarray(w_in[:, 2048:3584])
    wb = w_in[:, 4096:4112].reshape(D, 2, 8)
    wa = w_in[:, 4112:4128].reshape(D, 2, 8)
    if flip:
        wb, wa = wb[:, ::-1], wa[:, ::-1]
    m["wba"] = np.ascontiguousarray(np.concatenate([wb.reshape(D, 16), wa.reshape(D, 16)], axis=1))
    m["wz"] = np.ascontiguousarray(np.concatenate([w_in[:, 1536:2048], w_in[:, 3584:4096], w_in[:, 4128:6176]], axis=1))
    cw = inp["conv_w"][0]
    cw = cw[::-1] if flip else cw
    m["conv_wT"] = np.ascontiguousarray(cw.T.reshape(12, 128, 3).transpose(1, 0, 2))
    m["qk_normT"] = np.ascontiguousarray(np.stack([np.tile(inp["na_q_norm"][0], 2), np.tile(inp["na_k_norm"][0], 2)], axis=1))
    m["dnnorm_b"] = np.ascontiguousarray(np.broadcast_to(inp["dn_norm_w"][0][None, :], (128, 64)))
    al, dtb = inp["dn_A_log"][0], inp["dn_dt_bias"][0]
    if flip:
        al, dtb = al[::-1], dtb[::-1]
    m["alog_b"] = np.ascontiguousarray(np.broadcast_to(al.reshape(1, 16), (128, 16)))
    m["dtb_b"] = np.ascontiguousarray(np.broadcast_to(dtb.reshape(1, 16), (128, 16)))
    m["ropet"] = rope_tables(flip)
    m["nabias"] = na_bias_tables(inp["na_rpb"][0], flip)
    m["w_o_na"] = inp["w_o_na"][0]
    m["w_o_dn"] = inp["w_o_dn"][0]
    m["w_out"] = inp["w_out"][0]
    m["cf"] = cf
    m["lm"] = lm
    return m


IN_SHAPES = {
    "xv": ([T, D], F32), "ctxv": ([CTX, D], F32), "cT": ([128, 8, 2], F32), "mod_w": ([D, 3072], F32),
    "mod_bT": ([128, 24], F32), "modb_gate": ([128, 1024], F32), "norm_wT": ([128, 8], F32),
    "wna": ([D, 1536], F32), "wdn": ([D, 1536], F32), "wba": ([D, 32], F32), "wz": ([D, 3072], F32),
    "conv_wT": ([128, 12, 3], F32), "qk_normT": ([128, 2], F32), "dnnorm_b": ([128, 64], F32),
    "alog_b": ([128, 16], F32), "dtb_b": ([128, 16], F32), "ropet": ([128, 4, 128], F32),
    "nabias": ([128, 104, 128], F32), "w_o_na": ([512, D], F32), "w_o_dn": ([512, D], F32), "w_out": ([D, D], F32),
    "cf": ([128, len(CF), 128], F32), "lm": ([128, len(LEVELS), 128], U32),
}


def b3(ap2, n):
    return ap2.unsqueeze(2).to_broadcast([ap2.shape[0], ap2.shape[1], n])


def bm(ap2, k):
    return ap2.unsqueeze(1).to_broadcast([ap2.shape[0], k, ap2.shape[1]])


class Ctx:
    pass


class StopBuild(Exception):
    pass


PHASE_IN = {
    0: ["cf", "lm", "cT", "mod_w", "mod_bT", "modb_gate", "norm_wT"],
    1: ["cf", "lm", "ctxv", "wdn", "wba", "conv_wT", "alog_b", "dtb_b", "dnnorm_b"],
    2: ["cf", "lm", "xv", "wdn", "wba", "conv_wT", "alog_b", "dtb_b", "dnnorm_b", "ropet"],
    3: ["cf", "lm", "xv", "wdn", "wba", "conv_wT", "alog_b", "dtb_b", "dnnorm_b", "ropet"],
    4: ["cf", "lm", "xv", "ctxv", "wna", "qk_normT", "nabias"],
    5: ["cf", "lm", "xv", "wz", "w_o_na", "w_o_dn", "w_out"],
}


def in_shapes(phases, dbg_in=()):
    sh = dict(IN_SHAPES)
    sh["xv"] = ([T, D], F32)
    names = []
    for ph in phases:
        for k in PHASE_IN[ph]:
            if k not in names:
                names.append(k)
    out = {k: sh[k] for k in names}
    for k in dbg_in:
        out[k] = DBG_IN[k] if DBG_IN[k] is not None else (([512, OWN], F32) if k == "ona_in" else ([OWN, 512], F32))
    return out


DBG_IN = {"ab_in": ([128, 4, 8], F32), "gate_in": ([128, 1024], F32), "Sx_in": ([128, 2, 512], F32),
          "ona_in": None, "odn_in": None}


def build(phases=(0, 1, 2, 3, 4, 5), dbg=None, dbg_in=(), stop=None):
    nc = bass.Bass("TRN2", target_bir_lowering=False)
    nblk = OWN // 512
    nblk_all = T // 512
    I = {k: nc.dram_tensor(k, s, d, kind="ExternalInput").ap() for k, (s, d) in in_shapes(phases, dbg_in).items()}
    out_d = nc.dram_tensor("out", [OWN, D], F32, kind="ExternalOutput").ap()
    dbg = dbg or {}
    skind = "ExternalOutput" if "scr" in dbg else "Internal"
    of_d = nc.dram_tensor("of_scr", [OWN, 512], F32, kind=skind).ap()
    odn_d = nc.dram_tensor("odn_scr", [OWN, 512], F32, kind=skind).ap()
    ona_d = nc.dram_tensor("ona_scr", [512, OWN], F32, kind=skind).ap()
    dbg = {k: v for k, v in dbg.items() if k != "scr"}
    dbg_d = {k: nc.dram_tensor("dbg_" + k, list(s), F32, kind="ExternalOutput").ap() for k, s in dbg.items()}

    with contextlib.ExitStack() as outer:
        def sbo(n, s, d=F32):
            return outer.enter_context(nc.sbuf_tensor("g_" + n, s, d))
        G = Ctx()
        G.cf = sbo("cf", [128, len(CF), 128])
        G.lm = sbo("lm", [128, len(LEVELS), 128], U32)
        G.c = {k: G.cf[:, i, :] for i, k in enumerate(CF)}
        G.ab = sbo("ab", [128, 4, 8])
        G.gate = sbo("gate_bc", [128, 1024])
        G.Sx = [sbo("Sx%d" % d, [128, 4, 2, 64]) for d in range(2)]
        G.cb = sbo("cb", [128, 8])
        G.pp = [outer.enter_context(nc.psum_tensor("pp%d" % i, [128, 1024], F32)) for i in range(2)]
        G.pb = [outer.enter_context(nc.psum_tensor("pb%d" % i, [128, 512], F32)) for i in range(4)]
        G.ppi = 0
        G.pbi = 0

        def pair():
            G.ppi += 1
            return G.pp[G.ppi % 2]

        def bank():
            G.pbi += 1
            return G.pb[G.pbi % 4]

        pool = {"stack": outer, "dsem": {}, "dcount": {}, "phase": 0}

        def run_phase(fn):
            with contextlib.ExitStack() as st:
                P = Prog(nc, pool)

                def sb(n, s, d=F32):
                    return st.enter_context(nc.sbuf_tensor("p%d_%s" % (pool["phase"], n), s, d))
                try:
                    fn(P, sb)
                except StopBuild:
                    pass
                P.emit()
            nc.all_engine_barrier()

        def chk(n):
            if stop is not None and n >= stop:
                raise StopBuild()

        def dump(P, name, ap):
            if name in dbg_d:
                P.dma(dbg_d[name], ap)

        def load_w(P, stg, dst, src, nk, ncols, scale_bc=None):
            srcv = src.rearrange("(kc p) n -> p kc n", p=128)
            for i, c0 in enumerate(range(0, ncols, 256)):
                w = min(256, ncols - c0)
                s = stg[i % 2]
                P.dma(s[:, 0:nk, 0:w], srcv[:, :, c0:c0 + w])
                E = "dve" if i % 2 == 0 else "pool"
                if scale_bc is None:
                    P.cp(E, dst[:, :, c0:c0 + w], s[:, 0:nk, 0:w])
                else:
                    P.tt(E, dst[:, :, c0:c0 + w], s[:, 0:nk, 0:w], bm(scale_bc[:, c0:c0 + w], nk), ALU.mult)

        def ht_block(P, B, src_d, tok0, NT, seqlen, mod, halo=True, keep_x=None):
            N = NT * 128
            aT, bT = G.ab[:, 2 * mod, :], G.ab[:, 2 * mod + 1, :]
            for t in range(NT):
                xt = keep_x[:, t, :] if keep_x is not None else B.x[t % 2][:]
                xn = B.xn[t % 2][:] if keep_x is not None else xt
                P.dma(xt, src_d[tok0 + t * 128: tok0 + (t + 1) * 128, :])
                P.actv(B.junk[:], xt, AF.Square, accum_out=B.ss[:, 0:1])
                P.actv(B.ss[:, 1:2], B.ss[:, 0:1], AF.Ln, scale=1.0 / D, bias=G.cb[:, 0:1])
                P.actv(B.ss[:, 2:3], B.ss[:, 1:2], AF.Exp, scale=-0.5)
                P.actv(xn, xt, AF.Identity, scale=B.ss[:, 2:3])
                pp = pair()
                for kc in range(8):
                    P.tr(pp[:, kc * 128:(kc + 1) * 128], xn[:, kc * 128:(kc + 1) * 128], G.c["ident"])
                for kc in range(8):
                    o = B.hT[:, kc, 1 + t * 128: 1 + (t + 1) * 128]
                    if kc % 2 == 0:
                        P.actv(o, pp[:, kc * 128:(kc + 1) * 128], AF.Identity, scale=aT[:, kc:kc + 1], bias=bT[:, kc:kc + 1])
                    else:
                        P.ts("dve", o, pp[:, kc * 128:(kc + 1) * 128], aT[:, kc:kc + 1], bT[:, kc:kc + 1], ALU.mult, ALU.add)
            if not halo:
                return
            left, right = tok0 - 1, tok0 + N
            lo, ro = max(left, 0), min(right, seqlen - 1)
            P.dma(B.xh[0:1, :], src_d[lo:lo + 1, :])
            P.dma(B.xh[1:2, :], src_d[ro:ro + 1, :])
            P.actv(B.junk[0:2, :], B.xh[:], AF.Square, accum_out=B.ssh[:, 0:1])
            P.actv(B.ssh[:, 1:2], B.ssh[:, 0:1], AF.Ln, scale=1.0 / D, bias=G.cb[0:2, 0:1])
            P.actv(B.ssh[:, 2:3], B.ssh[:, 1:2], AF.Exp, scale=-0.5)
            P.actv(B.xh[:], B.xh[:], AF.Identity, scale=B.ssh[:, 2:3])
            pb = bank()
            for kc in range(8):
                P.tr(pb[:, kc * 2:(kc + 1) * 2], B.xh[0:2, kc * 128:(kc + 1) * 128], G.c["ident"][0:2, 0:2])
            pv = pb[:, 0:16].rearrange("p (k t) -> p k t", t=2)
            P.tt("dve", B.hh[:], pv, b3(aT, 2), ALU.mult)
            hv = B.hT[:, :, 0:N + 2:N + 1]
            P.tt("dve", hv, B.hh[:], b3(bT, 2), ALU.add)
            if left < 0:
                P.memset("dve", B.hT[:, :, 0:1], 0.0)
            if right > seqlen - 1:
                P.memset("dve", B.hT[:, :, N + 1:N + 2], 0.0)

        def proj_fm(P, B, W, c0, N, hoff=1):
            pb = bank()
            for kc in range(8):
                P.mm(pb[:, 0:N], W[:, kc, c0:c0 + 128], B.hT[:, kc, hoff:hoff + N], start=(kc == 0), stop=(kc == 7))
            return pb

        def phase0(P, sb):
            P.dma(G.cf[:], I["cf"])
            P.dma(G.lm[:], I["lm"])
            P.memset("dve", G.cb[:, 0:1], EPS)
            P.memset("dve", G.cb[:, 1:2], 1.0)
            P.memset("dve", G.cb[:, 2:3], math.log(0.125))
            P.memset("dve", G.cb[:, 3:4], 0.0)
            for d in range(2):
                P.memset("pool", G.Sx[d][:], 0.0)
            cT = sb("cT", [128, 8, 2]); scT = sb("scT", [128, 8, 2]); scB = sb("scB", [128, 8, 128])
            mbT = sb("mbT", [128, 24]); mbg = sb("mbg", [128, 1024]); nwT = sb("nwT", [128, 8])
            modT = sb("modT", [128, 16, 2])
            stg = [sb("mstg%d" % i, [128, 8, 512]) for i in range(2)]
            P.dma(cT[:], I["cT"]); P.dma(mbT[:], I["mod_bT"]); P.dma(mbg[:], I["modb_gate"]); P.dma(nwT[:], I["norm_wT"])
            P.actv(scT[:], cT[:], AF.Silu)
            P.cp("dve", scB[:], b3(scT[:, :, 0], 128))
            mw = I["mod_w"].rearrange("(kc p) n -> p kc n", p=128)
            for q in range(6):
                s = stg[q % 2]
                P.dma(s[:], mw[:, :, q * 512:(q + 1) * 512])
                if q < 4:
                    for ch in range(4):
                        pb = bank()
                        for kc in range(8):
                            P.mm(pb[:, 0:2], s[:, kc, ch * 128:(ch + 1) * 128], scT[:, kc, :], start=(kc == 0), stop=(kc == 7))
                        cg = q * 4 + ch
                        P.ts("dve", modT[:, cg, :], pb[:, 0:2], mbT[:, cg:cg + 1], None, ALU.add)
                else:
                    pb = bank()
                    for kc in range(8):
                        P.mm(pb[:, 0:512], scB[:, kc, :], s[:, kc, :], start=(kc == 0), stop=(kc == 7))
                    P.tt("dve", G.gate[:, (q - 4) * 512:(q - 3) * 512], pb[:, 0:512], mbg[:, (q - 4) * 512:(q - 3) * 512], ALU.add)
            for m_ in range(2):
                P.ts("dve", G.ab[:, 2 * m_, :], modT[:, 8:16, m_], 1.0, None, ALU.add)
                P.tt("dve", G.ab[:, 2 * m_, :], G.ab[:, 2 * m_, :], nwT[:], ALU.mult)
                P.cp("dve", G.ab[:, 2 * m_ + 1, :], modT[:, 0:8, m_])
            dump(P, "ab", G.ab[:])
            dump(P, "gate", G.gate[:])

        def gdn_buffers(P, sb):
            B = Ctx()
            B.x = [sb("x%d" % i, [128, D]) for i in range(2)]
            B.junk = sb("junk", [128, D], BF16)
            B.ss = sb("ss", [128, 4]); B.xh = sb("xh", [2, D]); B.ssh = sb("ssh", [2, 4]); B.hh = sb("hh", [128, 8, 2])
            B.hT = sb("hT", [128, 8, 514], BF16)
            B.wdn = sb("wdn", [128, 8, 1536], BF16)
            B.wba = sb("wba", [128, 8, 32], BF16)
            B.convw = sb("convw", [128, 12, 3])
            B.negA = sb("negA", [128, 16]); B.dtb = sb("dtb", [128, 16])
            B.raw = [sb("raw%d" % i, [128, 514]) for i in range(2)]
            B.acc = [sb("acc%d" % i, [128, 512]) for i in range(2)]
            B.sq = B.acc[1]; B.lnt = sb("lnt", [128, 512]); B.rstd = B.lnt; B.tmp = B.acc[0]
            B.cos = sb("cos", [128, 512]); B.sin = sb("sin", [128, 512])
            B.qT = sb("qT", [128, 4, 512]); B.kT = sb("kT", [128, 4, 512]); B.vT = sb("vT", [128, 4, 512])
            B.stg = [B.qT[:].rearrange("p a (b c) -> p (a b) c", c=256), B.kT[:].rearrange("p a (b c) -> p (a b) c", c=256)]
            B.bc = sb("bc", [128, 4, 8]); B.gcol = sb("gcol", [128, 4, 8]); B.bg = sb("bgt", [128, 4, 8])
            B.vnew = sb("vnew", [128, 8, 64]); B.o1 = sb("o1", [128, 8, 64]); B.o = sb("o", [128, 8, 64])
            B.TB = []
            for j in range(2):
                TB = Ctx()
                TB.Kt = sb("Kt%d" % j, [128, 8, 64]); TB.Vt = sb("Vt%d" % j, [128, 8, 64]); TB.bV = TB.Vt
                TB.kd = TB.Kt; TB.Kbz = sb("Kbz%d" % j, [128, 8, 2, 64]); TB.U = sb("U%d" % j, [128, 8, 64])
                TB.gcs = sb("gcs%d" % j, [128, 16]); TB.sm = sb("tsm%d" % j, [128, 40]); TB.on = sb("ton%d" % j, [128, 16])
                TB.kbT = sb("kbT%d" % j, [128, 4, 128])
                TB.kz = [sb("kz%d_%d" % (j, i), [128, 4, 128]) for i in range(2)]
                TB.qz = [sb("qz%d_%d" % (j, i), [128, 4, 128]) for i in range(2)]
                TB.WT = sb("WT%d" % j, [128, 4, 128])
                TB.big = [sb("big%d_%d" % (j, i), [128, 8, 128], F32 if i == 3 else GDT) for i in range(7)]
                TB.t2 = sb("t2_%d" % j, [128, 8, 128])
                TB.t1 = TB.big[3]
                TB.rb = TB.t2[:, 0:4, :]
                B.TB.append(TB)
            B.of = sb("ofl", [128, 8, 64]); B.osq = B.o1
            B.sm = sb("sm", [128, 40]); B.on = sb("on", [128, 16])
            B.dnw = sb("dnw", [128, 64])
            B.ropet = sb("ropet", [128, 4, 128])
            if "ropet" in I:
                P.dma(B.ropet[:], I["ropet"])
            load_w(P, B.stg, B.wdn, I["wdn"], 8, 1536)
            load_w(P, B.stg, B.wba, I["wba"], 8, 32)
            P.dma(B.convw[:], I["conv_wT"]); P.dma(B.dtb[:], I["dtb_b"]); P.dma(B.negA[:], I["alog_b"]); P.dma(B.dnw[:], I["dnnorm_b"])
            P.actv(B.negA[:], B.negA[:], AF.Exp)
            P.ts("dve", B.negA[:], B.negA[:], -1.0, None, ALU.mult)
            for TB in B.TB:
                for i in range(2):
                    P.memset("pool", TB.kz[i][:], 0.0)
                    P.memset("pool", TB.qz[i][:], 0.0)
                P.memset("pool", TB.Kbz[:], 0.0)
            return B

        def gdn_stage1(P, B, src_d, tok0, NT, seqlen, mod, need_q, rope, dirs):
            N = NT * 128
            ht_block(P, B, src_d, tok0, NT, seqlen, mod)
            chk(1)
            if rope:
                r0 = tok0 // 64
                ohr = G.c["ident"][:, r0:r0 + 8].unsqueeze(2).to_broadcast([128, 8, 64])
                ohc = G.c["ident"][:, 0:64].unsqueeze(1).to_broadcast([128, 8, 64])
                for i_, dstt in enumerate((B.cos, B.sin)):
                    pb = bank()
                    pv_ = pb[:, 0:512].rearrange("p (r c) -> p r c", r=8)
                    P.mm(pv_, B.ropet[:, 2 * i_, :], ohr, start=True, stop=False)
                    P.mm(pv_, B.ropet[:, 2 * i_ + 1, :], ohc, start=False, stop=True)
                    P.cp("act", dstt[:, 0:512], pb[:, 0:512])
            groups = list(range(12)) if need_q else list(range(4, 12))
            dst = lambda gi: (B.qT, B.kT, B.vT)[gi // 4][:, gi % 4, 0:N]
            for n_, gi in enumerate(groups):
                pb = proj_fm(P, B, B.wdn, gi * 128, N)
                ph = bank()
                for kc in range(8):
                    P.mm(ph[:, 0:2], B.wdn[:, kc, gi * 128:(gi + 1) * 128], B.hT[:, kc, 0:N + 2:N + 1], start=(kc == 0), stop=(kc == 7))
                raw = B.raw[n_ % 2]
                acc = B.acc[n_ % 2]
                P.cp("act", raw[:, 1:N + 1], pb[:, 0:N])
                P.cp("act", raw[:, 0:N + 2:N + 1], ph[:, 0:2])
                E = "dve"
                P.ts(E, acc[:, 0:N], raw[:, 0:N], B.convw[:, gi, 0:1], None, ALU.mult)
                P.stt(acc[:, 0:N], raw[:, 1:N + 1], B.convw[:, gi, 1:2], acc[:, 0:N], ALU.mult, ALU.add)
                P.stt(acc[:, 0:N], raw[:, 2:N + 2], B.convw[:, gi, 2:3], acc[:, 0:N], ALU.mult, ALU.add)
                P.actv(dst(gi), acc[:, 0:N], AF.Silu)
            chk(2)
            for gi in groups:
                if gi >= 8:
                    continue
                s = dst(gi)
                P.tt("pool", B.sq[:, 0:N], s, s, ALU.mult)
                pb = bank()
                P.mm(pb[:, 0:N], G.c["bd64"], B.sq[:, 0:N])
                P.actv(B.lnt[:, 0:N], pb[:, 0:N], AF.Ln, bias=G.cb[:, 0:1])
                P.actv(B.rstd[:, 0:N], B.lnt[:, 0:N], AF.Exp, scale=-0.5, bias=(G.cb[:, 2:3] if gi < 4 else G.cb[:, 3:4]))
                P.tt("dve", s, s, B.rstd[:, 0:N], ALU.mult)
                if rope:
                    pr_ = bank()
                    P.mm(pr_[:, 0:N], G.c["rp"], s)
                    P.tt("dve", B.tmp[:, 0:N], pr_[:, 0:N], B.sin[:, 0:N], ALU.mult)
                    P.tt("pool", s, s, B.cos[:, 0:N], ALU.mult)
                    P.tt("dve", s, s, B.tmp[:, 0:N], ALU.add)
            chk(3)
            for t in range(NT):
                pb = bank()
                for kc in range(8):
                    P.mm(pb[:, 0:32], B.hT[:, kc, 1 + t * 128:1 + (t + 1) * 128], B.wba[:, kc, :], start=(kc == 0), stop=(kc == 7))
                for di, d in enumerate(dirs):
                    bco = B.bc[:, t, :] if di == 0 else B.bc2[:, t, :]
                    gco = B.gcol[:, t, :] if di == 0 else B.gcol2[:, t, :]
                    P.actv(B.sm[:, 0:8], pb[:, d * 8:(d + 1) * 8], AF.Exp, scale=-1.0)
                    P.ts("dve", B.sm[:, 8:16], B.sm[:, 0:8], 1.0, None, ALU.add)
                    P.recip(bco, B.sm[:, 8:16])
                    P.tt("dve", B.sm[:, 16:24], pb[:, 16 + d * 8:16 + (d + 1) * 8], B.dtb[:, d * 8:(d + 1) * 8], ALU.add)
                    P.actv(B.sm[:, 24:32], B.sm[:, 16:24], AF.Exp)
                    P.actv(B.sm[:, 32:40], B.sm[:, 24:32], AF.Ln, bias=G.cb[:, 1:2])
                    P.tt("dve", gco, B.sm[:, 32:40], B.negA[:, d * 8:(d + 1) * 8], ALU.mult)

        def gdn_tile_pre(P, B, TB, d, t, full, bc, gcol):
            chk(4)
            cs = slice(t * 128, (t + 1) * 128)
            c = G.c
            big = TB.big
            pb = bank()
            for pr in range(4):
                P.tr(pb[:, pr * 128:(pr + 1) * 128], B.kT[:, pr, cs], c["ident"])
            P.cp("act", TB.Kt[:].rearrange("p h d -> p (h d)"), pb[:, 0:512])
            pb = bank()
            for pr in range(4):
                P.tr(pb[:, pr * 128:(pr + 1) * 128], B.vT[:, pr, cs], c["ident"])
            P.cp("act", TB.Vt[:].rearrange("p h d -> p (h d)"), pb[:, 0:512])
            chk(5)
            yield
            pb = bank()
            P.mm(pb[:, 0:8], c["tri%d" % d], gcol)
            P.mm(pb[:, 8:16], c["ones"], gcol)
            P.cp("dve", TB.gcs[:], pb[:, 0:16])
            gc, gsum = TB.gcs[:, 0:8], TB.gcs[:, 8:16]
            egc, egl, gtot, beg, tmp8 = TB.on[:, 0:8], TB.sm[:, 0:8], TB.sm[:, 8:16], TB.sm[:, 16:24], TB.sm[:, 24:32]
            P.actv(egc, gc, AF.Exp)
            P.tt("dve", tmp8, gsum, gc, ALU.subtract)
            P.actv(egl, tmp8, AF.Exp)
            P.actv(gtot, gsum, AF.Exp)
            P.tt("dve", beg, bc, egc, ALU.mult)
            chk(6)
            yield
            for hh in range(2):
                P.tt("pool", TB.rb, bm(c["ident"], 4), b3(bc[:, hh::2], 128), ALU.mult)
                pb = bank()
                P.mm(pb[:, 0:512], c["ones"], TB.rb.rearrange("p a b -> p (a b)"))
                hs = slice(hh * 64, (hh + 1) * 64)
                P.tt("dve", TB.kbT[hs], B.kT[hs, :, cs], pb[hs, 0:512].rearrange("p (a b) -> p a b", a=4), ALU.mult)
                P.cp("pool", TB.kz[hh][hs], B.kT[hs, :, cs])
                if full:
                    P.cp("pool", TB.qz[hh][hs], B.qT[hs, :, cs])
            chk(7)
            yield
            t1, t2 = TB.t1, TB.t2
            P.tt("pool", t1[:], bm(c["tri%d" % d], 8), b3(gcol, 128), ALU.mult)
            pp = pair()
            P.mm(pp[:, 0:512], c["ones"], t1[:, 0:4, :].rearrange("p a b -> p (a b)"))
            P.mm(pp[:, 512:1024], c["ones"], t1[:, 4:8, :].rearrange("p a b -> p (a b)"))
            ppv = lambda x: x[:, 0:1024].rearrange("p (h f) -> p h f", h=8)
            P.tt("dve", t1[:], ppv(pp), b3(gc, 128), ALU.subtract)
            P.ts("dve", t2[:], t1[:], -1.0, 0.0, ALU.mult, ALU.min)
            P.actv(t2[:], t2[:], AF.Exp)
            P.tt("pool", big[1][:], t2[:], bm(c["ms%d" % d], 8), ALU.mult)
            P.ts("dve", t2[:], t1[:], 0.0, None, ALU.min)
            P.actv(t2[:], t2[:], AF.Exp)
            if full:
                P.tt("pool", big[3][:], t2[:], bm(c["mi%d" % (1 - d)], 8), ALU.mult)
            P.tt("dve", big[2][:], t2[:], bm(c["ms%d" % (1 - d)], 8), ALU.mult)
            chk(8)
            yield
            pp = pair()
            for h in range(8):
                P.mm(pp[:, h * 128:(h + 1) * 128], TB.kbT[:, h // 2, :], TB.kz[h % 2][:, h // 2, :])
            P.tt("dve", big[1][:], ppv(pp), big[1][:], ALU.mult)
            yield
            pp = pair()
            for h in range(8):
                P.mm(pp[:, h * 128:(h + 1) * 128], TB.kz[h % 2][:, h // 2, :], TB.kbT[:, h // 2, :])
            P.tt("dve", big[2][:], ppv(pp), big[2][:], ALU.mult)
            yield
            if full:
                pp = pair()
                for h in range(8):
                    P.mm(pp[:, h * 128:(h + 1) * 128], B.kT[:, h // 2, cs], TB.qz[h % 2][:, h // 2, :])
                P.tt("dve", big[3][:], ppv(pp), big[3][:], ALU.mult)
            L_, N_, QK_, X_, D_, Fn, F2n = big[1], big[2], big[3], big[4], big[5], big[0], big[6]
            chk(9)
            yield
            lmf = B.lm1f
            P.tt("dve", X_[:], L_[:], bm(lmf[:], 8), ALU.mult)
            P.tt("dve", X_[:], bm(c["ident"], 8), X_[:], ALU.subtract)
            P.tt("pool", D_[:], N_[:], bm(lmf[:], 8), ALU.mult)
            P.tt("pool", D_[:], bm(c["ident"], 8), D_[:], ALU.subtract)
            chk(10)
            yield
            for li in range(1, len(LEVELS)):
                last = li == len(LEVELS) - 1
                mask = bm(G.lm[:, li, :], 8)
                ppF = pair()
                for h in range(8):
                    P.mm(ppF[:, h * 128:(h + 1) * 128], L_[:, h, :], D_[:, h, :])
                if not last:
                    ppF2 = pair()
                    for h in range(8):
                        P.mm(ppF2[:, h * 128:(h + 1) * 128], N_[:, h, :], X_[:, h, :])
                P.actv(Fn[:], ppv(ppF), AF.Identity, scale=-1.0)
                if not last:
                    P.actv(F2n[:], ppv(ppF2), AF.Identity, scale=-1.0)
                yield
                ppG = pair()
                for h in range(8):
                    P.mm(ppG[:, h * 128:(h + 1) * 128], X_[:, h, :], Fn[:, h, :])
                if not last:
                    ppG2 = pair()
                    for h in range(8):
                        P.mm(ppG2[:, h * 128:(h + 1) * 128], D_[:, h, :], F2n[:, h, :])
                    P.cpred(X_[:], mask, ppv(ppG2))
                P.cpred(D_[:], mask, ppv(ppG))
                yield
            chk(11)
            yield
            P.tt("pool", TB.bV[:], TB.Vt[:], b3(bc, 64), ALU.mult)
            for hh in range(2):
                P.tt("pool", TB.Kbz[:, hh::2, hh, :], TB.Kt[:, hh::2, :], b3(beg[:, hh::2], 64), ALU.mult)
            P.tt("pool", TB.kd[:], TB.Kt[:], b3(egl, 64), ALU.mult)
            if GDT != F32:
                P.cp("act", TB.t2[:], D_[:])
                D_ = TB.t2
            pb = bank()
            for h in range(8):
                P.mm(pb[:, h * 64:(h + 1) * 64], D_[:, h, :], TB.bV[:, h, :])
            P.cp("act", TB.U[:].rearrange("p h d -> p (h d)"), pb[:, 0:512])
            pb = bank()
            for pr in range(4):
                for hh in range(2):
                    h = 2 * pr + hh
                    P.mm(pb[:, pr * 128:(pr + 1) * 128], TB.Kbz[:, h, :, :].rearrange("p a b -> p (a b)"), D_[:, h, :],
                         start=(hh == 0), stop=(hh == 1))
            P.cp("act", TB.WT[:].rearrange("p a b -> p (a b)"), pb[:, 0:512])
        def gdn_tile_step(P, B, TB, d, t, full, out_cb):
            cs = slice(t * 128, (t + 1) * 128)
            Sx = G.Sx[d]
            QK_ = TB.big[3]
            egc, gtot = TB.on[:, 0:8], TB.sm[:, 8:16]
            pb = bank()
            for pr in range(4):
                P.mm(pb[:, pr * 128:(pr + 1) * 128], TB.WT[:, pr, :], Sx[:, pr, :, :].rearrange("p a b -> p (a b)"))
            P.tt("dve", B.vnew[:].rearrange("p h d -> p (h d)"), TB.U[:].rearrange("p h d -> p (h d)"), pb[:, 0:512], ALU.subtract)
            if full:
                pb1 = bank()
                for pr in range(4):
                    P.mm(pb1[:, pr * 128:(pr + 1) * 128], B.qT[:, pr, cs], Sx[:, pr, :, :].rearrange("p a b -> p (a b)"))
                P.tt("dve", B.o1[:], pb1[:, 0:512].rearrange("p (h d) -> p h d", h=8), b3(egc, 64), ALU.mult)
                pb2 = bank()
                for h in range(8):
                    P.mm(pb2[:, h * 64:(h + 1) * 64], QK_[:, h, :], B.vnew[:, h, :])
                P.tt("dve", B.o[:].rearrange("p h d -> p (h d)"), B.o1[:].rearrange("p h d -> p (h d)"), pb2[:, 0:512], ALU.add)
            pbc = bank()
            for pr in range(4):
                P.mm(pbc[:, pr * 128:(pr + 1) * 128], TB.kd[:, 2 * pr:2 * pr + 2, :].rearrange("p a b -> p (a b)"),
                     B.vnew[:, 2 * pr:2 * pr + 2, :].rearrange("p a b -> p (a b)"))
            for hh in range(2):
                hs = slice(hh * 64, (hh + 1) * 64)
                sv = Sx[hs, :, hh, :]
                P.tt("pool", sv, sv, b3(gtot[hs, hh::2], 64), ALU.mult)
                cv = pbc[hs, 0:512].rearrange("p (a b d) -> p a b d", a=4, b=2)[:, :, hh, :]
                P.tt("dve", sv, sv, cv, ALU.add)
            if full:
                out_cb(t)


        def run_tiles(P, B, d, order, full, bcf, gcf, out_cb):
            for i in range(0, len(order), 2):
                grp = order[i:i + 2]
                gens = [gdn_tile_pre(P, B, B.TB[j], d, t, full, bcf(t), gcf(t)) for j, t in enumerate(grp)]
                alive = list(gens)
                while alive:
                    for g in list(alive):
                        try:
                            next(g)
                        except StopIteration:
                            alive.remove(g)
                for j, t in enumerate(grp):
                    gdn_tile_step(P, B, B.TB[j], d, t, full, out_cb)

        def gdn_phase_common(P, sb):
            B = gdn_buffers(P, sb)
            B.bc2 = sb("bc2", [128, 4, 8]); B.gcol2 = sb("gcol2", [128, 4, 8])
            B.lm1f = sb("lm1f", [128, 128])
            P.cp("dve", B.lm1f[:], G.lm[:, 0, :])
            return B

        def phase1(P, sb):
            B = gdn_phase_common(P, sb)
            gdn_stage1(P, B, I["ctxv"], 0, 2, CTX, 1, False, False, [0, 1])
            run_tiles(P, B, 0, [0, 1], False, lambda t: B.bc[:, t, :], lambda t: B.gcol[:, t, :], None)
            run_tiles(P, B, 1, [1, 0], False, lambda t: B.bc2[:, t, :], lambda t: B.gcol2[:, t, :], None)
            dump(P, "Sx0", G.Sx[0][:].rearrange("p a b d -> p (a b d)"))
            dump(P, "Sx1", G.Sx[1][:].rearrange("p a b d -> p (a b d)"))

        def phase2(P, sb):
            B = gdn_phase_common(P, sb)
            for ib in range(nblk):
                gdn_stage1(P, B, I["xv"], ib * 512, 4, T, 0, True, True, [0])

                def ocb(t, ib=ib):
                    r0 = ib * 512 + t * 128
                    P.dma(of_d[r0:r0 + 128, :], B.o[:].rearrange("p h d -> p (h d)"))
                run_tiles(P, B, 0, [0, 1, 2, 3], True, lambda t: B.bc[:, t, :], lambda t: B.gcol[:, t, :], ocb)
                if ib == 0:
                    dump(P, "kT0", B.kT[:].rearrange("p a b -> p (a b)"))
                    dump(P, "qT0", B.qT[:].rearrange("p a b -> p (a b)"))
                    dump(P, "vT0", B.vT[:].rearrange("p a b -> p (a b)"))
                    dump(P, "bc0", B.bc[:].rearrange("p a b -> p (a b)"))
                    dump(P, "gcol0", B.gcol[:].rearrange("p a b -> p (a b)"))

        def phase3(P, sb):
            B = gdn_phase_common(P, sb)
            if True:
                for ib in range(nblk_all - 1, nblk - 1, -1):
                    gdn_stage1(P, B, I["xv"], ib * 512, 4, T, 0, False, True, [1])
                    run_tiles(P, B, 1, [3, 2, 1, 0], False, lambda t: B.bc[:, t, :], lambda t: B.gcol[:, t, :], None)
            for ib in range(nblk - 1, -1, -1):
                gdn_stage1(P, B, I["xv"], ib * 512, 4, T, 0, True, True, [1])

                def ocb(t, ib=ib):
                    r0 = ib * 512 + t * 128
                    P.dma(B.of[:].rearrange("p h d -> p (h d)"), of_d[r0:r0 + 128, :])
                    P.tt("pool", B.o[:], B.o[:], B.of[:], ALU.add)
                    P.tt("pool", B.osq[:], B.o[:], B.o[:], ALU.mult)
                    P.reduce("dve", B.on[:, 8:16], B.osq[:], ALU.add)
                    P.actv(B.sm[:, 32:40], B.on[:, 8:16], AF.Ln, scale=1.0 / 64, bias=G.cb[:, 0:1])
                    P.actv(B.sm[:, 32:40], B.sm[:, 32:40], AF.Exp, scale=-0.5)
                    P.tt("dve", B.osq[:], B.o[:], b3(B.sm[:, 32:40], 64), ALU.mult)
                    P.tt("dve", B.osq[:], B.osq[:], bm(B.dnw[:], 8), ALU.mult)
                    P.dma(odn_d[r0:r0 + 128, :], B.osq[:].rearrange("p h d -> p (h d)"))
                run_tiles(P, B, 1, [3, 2, 1, 0], True, lambda t: B.bc[:, t, :], lambda t: B.gcol[:, t, :], ocb)

        def phase4(P, sb):
            B = Ctx()
            B.x = [sb("x%d" % i, [128, D]) for i in range(2)]
            B.junk = sb("junk", [128, D], BF16)
            B.ss = sb("ss", [128, 4])
            B.hT = sb("hT", [128, 8, 514], BF16)
            B.wna = sb("wna", [128, 8, 1536], BF16)
            B.stg = [sb("stg%d" % i, [128, 8, 256]) for i in range(2)]
            bias = sb("nabias", [128, 104, 128], BF16)
            identb = sb("identb", [128, 128], BF16); bd64b = sb("bd64b", [128, 128], BF16)
            ozb = [sb("ozb%d" % i, [128, 128], BF16) for i in range(2)]
            wq = sb("wqk", [128, 2])
            qz = [[sb("qz%d_%d" % (s, hh), [128, 4, 512], BF16) for hh in range(2)] for s in range(2)]
            kTr = [sb("kTr%d" % s, [128, 4, 512], BF16) for s in range(3)]
            Vz = [sb("Vz%d" % s, [128, 4, 8, 2, 64], BF16) for s in range(3)]
            kcT = sb("kcT", [128, 4, 256], BF16)
            Vzc = sb("Vzc", [128, 2, 8, 2, 64], BF16)
            sq = sb("nsq", [128, 512], BF16); lnt = sb("nlnt", [128, 512]); rstd = sb("nrstd", [128, 512])
            Eb = [sb("E%d" % i, [128, 896], BF16) for i in range(3)]
            rden = sb("rden", [128, 128])
            oT = [sb("oT%d" % i, [128, 4, 512]) for i in range(2)]
            load_w(P, B.stg, B.wna, I["wna"], 8, 1536)
            nbv = I["nabias"]
            for i, t0 in enumerate(range(0, 104, 16)):
                n = min(16, 104 - t0)
                s = B.stg[i % 2]
                sv = s[:].rearrange("p a b -> p (a b)")[:, 0:n * 128].rearrange("p (a b) -> p a b", b=128)
                P.dma(sv, nbv[:, t0:t0 + n, :])
                P.cp("dve" if i % 2 == 0 else "pool", bias[:, t0:t0 + n, :], sv)
            P.cp("dve", identb[:], G.c["ident"])
            P.ts("dve", bd64b[:], G.c["bd64"], 1.0 / 64, None, ALU.mult)
            for i in range(2):
                P.cp("dve", ozb[i][:], G.c["oz%d" % i])
            P.dma(wq[:], I["qk_normT"])
            P.ts("dve", wq[:, 0:1], wq[:, 0:1], 0.125, None, ALU.mult)
            for s in range(2):
                for hh in range(2):
                    P.memset("pool", qz[s][hh][:], 0.0)
            for s in range(3):
                P.memset("pool", Vz[s][:], 0.0)
            P.memset("pool", Vzc[:], 0.0)

            def qknorm(pb, N, which, outs):
                P.actv(sq[:, 0:N], pb[:, 0:N], AF.Square)
                p2 = bank()
                P.mm(p2[:, 0:N], bd64b[:], sq[:, 0:N])
                P.actv(lnt[:, 0:N], p2[:, 0:N], AF.Ln, bias=G.cb[:, 0:1])
                P.actv(rstd[:, 0:N], lnt[:, 0:N], AF.Exp, scale=-0.5)
                for (o, ps_) in outs:
                    P.stt(o, pb[ps_, 0:N], wq[ps_, which:which + 1], rstd[ps_, 0:N], ALU.mult, ALU.mult)

            def v_tiles(NT, dstfn):
                for t in range(NT):
                    pb = bank()
                    for kc in range(8):
                        P.mm(pb[:, 0:512], B.hT[:, kc, 1 + t * 128:1 + (t + 1) * 128], B.wna[:, kc, 1024:1536], start=(kc == 0), stop=(kc == 7))
                    pv = pb[:, 0:512].rearrange("p (h d) -> p h d", d=64)
                    for hh in range(2):
                        P.cp("act" if hh == 0 else "dve", dstfn(t)[:, hh::2, hh, :], pv[:, hh::2, :])

            ht_block(P, B, I["ctxv"], 0, 2, CTX, 1, halo=False)
            for g in range(4):
                pb = proj_fm(P, B, B.wna, 512 + g * 128, 256)
                qknorm(pb, 256, 1, [(kcT[:, g, :], slice(0, 128))])
            v_tiles(2, lambda t: Vzc[:, t])

            full = slice(0, 128)

            def attend(ib):
                ob = oT[ib % 2]
                for j in range(4):
                    qt = ib * 4 + j
                    if qt < 2:
                        kts = [0, 1, 2, 3]
                        tb = lambda n_, h: 40 + (qt * 4 + n_) * 8 + h
                    else:
                        kts = [qt - 2 + o for o in range(5)]
                        tb = lambda n_, h: n_ * 8 + h
                    ncol = (len(kts) + 2) * 128
                    for pr in range(4):
                        pps = []
                        for hh in range(2):
                            h = 2 * pr + hh
                            pp = pair()
                            q_ = qz[ib % 2][hh][:, pr, j * 128:(j + 1) * 128]
                            for n_, kt in enumerate(kts):
                                kb, lt = kt // 4, kt % 4
                                P.mm(pp[:, n_ * 128:(n_ + 1) * 128], kTr[kb % 3][:, pr, lt * 128:(lt + 1) * 128], q_, start=True, stop=False)
                                P.mm(pp[:, n_ * 128:(n_ + 1) * 128], identb[:], bias[:, tb(n_, h), :], start=False, stop=True)
                            for ct in range(2):
                                c0 = (len(kts) + ct) * 128
                                P.mm(pp[:, c0:c0 + 128], kcT[:, pr, ct * 128:(ct + 1) * 128], q_)
                            pps.append(pp)
                        Es = []
                        for hh in range(2):
                            G.ei = getattr(G, "ei", 0) + 1
                            E = Eb[G.ei % 3]
                            P.actv(E[:, 0:ncol], pps[hh][:, 0:ncol], AF.Exp)
                            Es.append(E)
                        pn, pd = bank(), bank()
                        nmm = 2 * (len(kts) + 2)
                        i_ = 0
                        for hh in range(2):
                            h = 2 * pr + hh
                            for n_ in range(len(kts) + 2):
                                if n_ < len(kts):
                                    kt = kts[n_]
                                    vv = Vz[(kt // 4) % 3][:, kt % 4, h, :, :]
                                else:
                                    vv = Vzc[:, n_ - len(kts), h, :, :]
                                e_ = Es[hh][:, n_ * 128:(n_ + 1) * 128]
                                P.mm(pn[:, 0:128], vv.rearrange("p a b -> p (a b)"), e_, start=(i_ == 0), stop=(i_ == nmm - 1))
                                P.mm(pd[:, 0:128], ozb[hh][:], e_, start=(i_ == 0), stop=(i_ == nmm - 1))
                                i_ += 1
                        P.recip(rden[:], pd[:, 0:128])
                        P.tt("dve", ob[:, pr, j * 128:(j + 1) * 128], pn[:, 0:128], rden[:], ALU.mult)
                onav = ona_d.rearrange("(pr p) t -> p pr t", p=128)
                P.dma(onav[:, :, ib * 512:(ib + 1) * 512], ob[:])
                if ib == 0:
                    dump(P, "onaT0", ob[:].rearrange("p a b -> p (a b)"))

            for ib in range(nblk + 1):
                ht_block(P, B, I["xv"], ib * 512, 4, T, 0, halo=False)
                if ib < nblk:
                    for g in range(4):
                        pb = proj_fm(P, B, B.wna, g * 128, 512)
                        qknorm(pb, 512, 0, [(qz[ib % 2][0][0:64, g, :], slice(0, 64)), (qz[ib % 2][1][64:128, g, :], slice(64, 128))])
                for g in range(4):
                    pb = proj_fm(P, B, B.wna, 512 + g * 128, 512)
                    qknorm(pb, 512, 1, [(kTr[ib % 3][:, g, :], full)])
                v_tiles(4, lambda t, ib=ib: Vz[ib % 3][:, t])
                if ib >= 1:
                    attend(ib - 1)

        def phase5(P, sb):
            B = Ctx()
            B.xn = [sb("xn%d" % i, [128, D]) for i in range(2)]
            xk = sb("xk", [128, 4, D])
            B.junk = sb("junk", [128, D], BF16)
            B.ss = sb("ss", [128, 4])
            B.hT = sb("hT", [128, 8, 514], BF16)
            B.stg = [sb("stg%d" % i, [128, 8, 256]) for i in range(2)]
            wz = sb("wz", [128, 8, 3072], BF16)
            wona = sb("wona", [128, 4, 1024], BF16); wodn = sb("wodn", [128, 4, 1024], BF16)
            wout = sb("wout", [128, 8, 1024], BF16)
            onaT = sb("onaT", [128, 4, 512]); odn = sb("odn", [128, 4, 512])
            AnaT = sb("AnaT", [128, 4, 512], BF16); AdnT = sb("AdnT", [128, 4, 512], BF16)
            yT = sb("yT", [128, 8, 512], BF16)
            sz = [sb("sz%d" % i, [128, 512]) for i in range(2)]
            s1 = sb("s1", [128, 512]); s2 = sb("s2", [128, 512]); t1 = sb("mt1", [128, 512]); t2 = sb("mt2", [128, 512])
            ot = [sb("ot%d" % i, [128, D]) for i in range(2)]
            load_w(P, B.stg, wz, I["wz"], 8, 3072)
            load_w(P, B.stg, wona, I["w_o_na"], 4, 1024)
            load_w(P, B.stg, wodn, I["w_o_dn"], 4, 1024)
            load_w(P, B.stg, wout, I["w_out"], 8, 1024, scale_bc=G.gate)
            onav = I.get("ona_in", ona_d).rearrange("(pr p) t -> p pr t", p=128)
            odn_src = I.get("odn_in", odn_d)
            for ib in range(nblk):
                ht_block(P, B, I["xv"], ib * 512, 4, T, 0, halo=False, keep_x=xk)
                P.dma(onaT[:], onav[:, :, ib * 512:(ib + 1) * 512])
                P.dma(odn[:], odn_src[ib * 512:(ib + 1) * 512, :].rearrange("(t p) f -> p t f", p=128))
                for g in range(4):
                    pb = proj_fm(P, B, wz, g * 128, 512)
                    P.actv(sz[g % 2][:], pb[:, 0:512], AF.Silu)
                    P.tt("pool", AnaT[:, g, :], onaT[:, g, :], sz[g % 2][:], ALU.mult)
                for g in range(4):
                    pb = proj_fm(P, B, wz, 512 + g * 128, 512)
                    P.actv(sz[g % 2][:], pb[:, 0:512], AF.Silu)
                    pt = bank()
                    for t in range(4):
                        P.tr(pt[:, t * 128:(t + 1) * 128], odn[:, t, g * 128:(g + 1) * 128], G.c["ident"])
                    P.tt("dve", AdnT[:, g, :], pt[:, 0:512], sz[g % 2][:], ALU.mult)
                for j in range(8):
                    pg1 = proj_fm(P, B, wz, 1024 + j * 128, 512)
                    P.actv(s1[:], pg1[:, 0:512], AF.Sigmoid)
                    pg2 = proj_fm(P, B, wz, 2048 + j * 128, 512)
                    P.actv(s2[:], pg2[:, 0:512], AF.Sigmoid)
                    pu1 = bank()
                    for fc in range(4):
                        P.mm(pu1[:, 0:512], wona[:, fc, j * 128:(j + 1) * 128], AnaT[:, fc, :], start=(fc == 0), stop=(fc == 3))
                    P.tt("dve", t1[:], pu1[:, 0:512], s1[:], ALU.mult)
                    pu2 = bank()
                    for fc in range(4):
                        P.mm(pu2[:, 0:512], wodn[:, fc, j * 128:(j + 1) * 128], AdnT[:, fc, :], start=(fc == 0), stop=(fc == 3))
                    P.tt("dve", t2[:], pu2[:, 0:512], s2[:], ALU.mult)
                    P.tt("pool", yT[:, j, :], t1[:], t2[:], ALU.add)
                for t in range(4):
                    o_ = ot[t % 2]
                    for hf in range(2):
                        pb = bank()
                        for kc in range(8):
                            P.mm(pb[:, 0:512], yT[:, kc, t * 128:(t + 1) * 128], wout[:, kc, hf * 512:(hf + 1) * 512], start=(kc == 0), stop=(kc == 7))
                        P.tt("dve", o_[:, hf * 512:(hf + 1) * 512], pb[:, 0:512], xk[:, t, hf * 512:(hf + 1) * 512], ALU.add)
                    r0 = ib * 512 + t * 128
                    P.dma(out_d[r0:r0 + 128, :], o_[:])

        def phase_dbgin(P, sb):
            P.dma(G.cf[:], I["cf"])
            P.dma(G.lm[:], I["lm"])
            P.memset("dve", G.cb[:, 0:1], EPS)
            P.memset("dve", G.cb[:, 1:2], 1.0)
            P.memset("dve", G.cb[:, 2:3], math.log(0.125))
            P.memset("dve", G.cb[:, 3:4], 0.0)
            for d in range(2):
                P.memset("pool", G.Sx[d][:], 0.0)
            if "ab_in" in I:
                P.dma(G.ab[:], I["ab_in"])
            if "gate_in" in I:
                P.dma(G.gate[:], I["gate_in"])
            if "Sx_in" in I:
                for d in range(2):
                    P.dma(G.Sx[d][:].rearrange("p a b d -> p (a b d)"), I["Sx_in"][:, d, :])

        fns = {0: phase0, 1: phase1, 2: phase2, 3: phase3, 4: phase4, 5: phase5}
        if 0 not in phases:
            run_phase(phase_dbgin)
        for ph in phases:
            run_phase(fns[ph])
    return nc


_CACHE = {}


def kernel(**inputs):
    inp = {k: np.asarray(v) for k, v in inputs.items()}
    if "nc" not in _CACHE:
        _CACHE["nc"] = build()
    nc = _CACHE["nc"]
    cf, lm = host_consts()
    in_maps = []
    for core in range(8):
        b, half = core // 2, core % 2
        m = prep_core(inp, b, half, cf, lm)
        in_maps.append({k: np.ascontiguousarray(m[k]) for k in IN_SHAPES})
    res = run_bass_kernel_spmd(nc, in_maps, core_ids=list(range(8)))
    out = np.empty((4, T, D), np.float32)
    for core in range(8):
        b, half = core // 2, core % 2
        o = res.results[core]["out"]
        if half == 0:
            out[b, 0:OWN] = o
        else:
            out[b, OWN:T] = o[::-1]
    return out
```

```python
import contextlib
import math
import numpy as np
import concourse.bass as bass
import concourse.mybir as mybir
from concourse.bass_utils import run_bass_kernel_spmd

F32 = mybir.dt.float32
BF16 = mybir.dt.bfloat16
U32 = mybir.dt.uint32
ALU = mybir.AluOpType
AF = mybir.ActivationFunctionType
AX = mybir.AxisListType

T = 8192
D = 1024
OWN = 4096
CTX = 256
EPS = 1e-6
GDT = BF16
NEG = -30000.0


def _tname(ap):
    t = getattr(ap, "tensor", None)
    return None if t is None else t.name


class Prog:
    ENG = ["pe", "dve", "act", "pool", "sp"]

    def __init__(self, nc, pool):
        self.nc = nc
        self.pool = pool
        self.q = {e: [] for e in self.ENG}
        self.lastw = {}
        self.readers = {}
        self.waited = {e: {} for e in self.ENG}
        self.dcount = dict(pool["dcount"])

    def _deps(self, E, reads, writes):
        evs = []
        raw = set()
        for r in reads:
            if r in self.lastw:
                evs.append(self.lastw[r])
                raw.add(self.lastw[r])
        for w in writes:
            if w in self.lastw:
                evs.append(self.lastw[w])
            evs += self.readers.get(w, [])
        waits = {}
        for ev in evs:
            if ev[0] == "c":
                if ev[1] == E and (E == "pe" or ev not in raw):
                    continue
                key = ("c", ev[1])
            else:
                key = ("d", ev[1])
            val = ev[2]
            if self.waited[E].get(key, -1) >= val:
                continue
            if waits.get(key, -1) < val:
                waits[key] = val
        for k, v in waits.items():
            self.waited[E][k] = v
            if k[0] == "c":
                self.q[k[1]][v]["inc"] = True
        return waits

    def _commit(self, ev, reads, writes):
        for r in reads:
            self.readers.setdefault(r, []).append(ev)
        for w in writes:
            self.lastw[w] = ev
            self.readers[w] = []

    def op(self, E, fn, reads, writes):
        reads = [r for r in reads if r is not None]
        writes = [w for w in writes if w is not None]
        waits = self._deps(E, reads, writes)
        idx = len(self.q[E])
        self.q[E].append(dict(fn=fn, waits=waits, inc=False, dma=None))
        self._commit(("c", E, idx), reads, writes)

    def dma(self, out, in_, Q="sp", reads=None, writes=None):
        on, inn = _tname(out), _tname(in_)
        reads = [inn] if reads is None else reads
        writes = [on] if writes is None else writes
        sb_out = "SB" in out.tensor.__class__.__name__
        kn = on if sb_out else inn
        if kn[0] == "p" and "_" in kn and kn[1:kn.index("_")].isdigit():
            kn = kn[kn.index("_") + 1:]
        key = "%s:%s" % (Q, kn)
        waits = self._deps(Q, reads, writes)
        cnt = self.dcount.get(key, 0) + 16
        self.dcount[key] = cnt
        self.q[Q].append(dict(fn=lambda e: e.dma_start(out=out, in_=in_), waits=waits, inc=False, dma=key))
        self._commit(("d", key, cnt), reads, writes)

    def _rw(self, out, ins, extra_w=()):
        w = [_tname(out)] + [_tname(a) for a in extra_w]
        r = [_tname(a) for a in ins if hasattr(a, "tensor")]
        return r, w

    def mm(self, out, lhsT, rhs, start=True, stop=True):
        r, w = self._rw(out, [lhsT, rhs])
        self.op("pe", lambda e: e.matmul(out, lhsT, rhs, start=start, stop=stop), r, w)

    def tr(self, out, in_, ident):
        r, w = self._rw(out, [in_, ident])
        self.op("pe", lambda e: e.transpose(out, in_, ident), r, w)

    def tt(self, E, out, in0, in1, op):
        r, w = self._rw(out, [in0, in1])
        self.op(E, lambda e: e.tensor_tensor(out, in0, in1, op), r, w)

    def ts(self, E, out, in0, s1, s2, op0, op1=None):
        ins = [in0] + [s for s in (s1, s2) if hasattr(s, "tensor")]
        r, w = self._rw(out, ins)
        if op1 is None:
            self.op(E, lambda e: e.tensor_scalar(out, in0, s1, None, op0), r, w)
        else:
            self.op(E, lambda e: e.tensor_scalar(out, in0, s1, s2, op0, op1), r, w)

    def stt(self, out, in0, scalar, in1, op0, op1):
        ins = [in0, in1] + ([scalar] if hasattr(scalar, "tensor") else [])
        r, w = self._rw(out, ins)
        self.op("dve", lambda e: e.scalar_tensor_tensor(out, in0, scalar, in1, op0, op1), r, w)

    def cp(self, E, out, in_):
        r, w = self._rw(out, [in_])
        if E == "act":
            self.op(E, lambda e: e.copy(out, in_), r, w)
        else:
            self.op(E, lambda e: e.tensor_copy(out, in_), r, w)

    def actv(self, out, in_, func, bias=None, scale=None, accum_out=None):
        ins = [in_] + [s for s in (bias, scale) if hasattr(s, "tensor")]
        r, w = self._rw(out, ins, [accum_out] if accum_out is not None else [])
        kw = {}
        if bias is not None:
            kw["bias"] = bias
        if scale is not None:
            kw["scale"] = scale
        if accum_out is not None:
            kw["accum_out"] = accum_out
        self.op("act", lambda e: e.activation(out, in_, func, **kw), r, w)

    def recip(self, out, in_):
        r, w = self._rw(out, [in_])
        self.op("dve", lambda e: e.reciprocal(out, in_), r, w)

    def memset(self, E, ap, val):
        r, w = self._rw(ap, [])
        self.op(E, lambda e: e.memset(ap, val), r, w)

    def reduce(self, E, out, in_, op):
        r, w = self._rw(out, [in_])
        self.op(E, lambda e: e.tensor_reduce(out, in_, AX.X, op), r, w)

    def cpred(self, out, mask, data):
        r, w = self._rw(out, [mask, data, out])
        self.op("dve", lambda e: e.copy_predicated(out, mask, data), r, w)

    def emit(self):
        nc = self.nc
        pool = self.pool
        with contextlib.ExitStack() as st:
            pool["phase"] += 1
            if "csem" not in pool:
                pool["csem"] = {e: pool["stack"].enter_context(nc.semaphore("cs_%s" % e)) for e in self.ENG}
                pool["cbase"] = {e: 0 for e in self.ENG}
            csem = pool["csem"]
            cbase = dict(pool["cbase"])
            dsem = pool["dsem"]
            for k in sorted(self.dcount):
                if k not in dsem:
                    dsem[k] = pool["stack"].enter_context(nc.semaphore("ds%d" % len(dsem)))
            cum = {}
            for e in self.ENG:
                c = 0
                arr = []
                for rec in self.q[e]:
                    if rec["inc"] and rec["dma"] is None:
                        c += 1
                    arr.append(c)
                cum[e] = arr
            block = st.enter_context(nc.Block())
            engobj = {"pe": block.tensor, "dve": block.vector, "act": block.scalar,
                      "pool": block.gpsimd, "sp": block.sync}
            for e in self.ENG:
                recs = self.q[e]
                my_dma = {}
                for rec in recs:
                    if rec["dma"] is not None:
                        my_dma[rec["dma"]] = self.dcount[rec["dma"]]

                def body(eng, recs=recs, e=e, my_dma=my_dma):
                    for rec in recs:
                        for k, v in rec["waits"].items():
                            if k[0] == "c":
                                eng.wait_ge(csem[k[1]], cbase[k[1]] + cum[k[1]][v])
                            else:
                                eng.wait_ge(dsem[k[1]], v)
                        ins = rec["fn"](eng)
                        if rec["dma"] is not None:
                            ins.then_inc(dsem[rec["dma"]], 16)
                        elif rec["inc"]:
                            ins.then_inc(csem[e], 1)
                    for k, v in my_dma.items():
                        eng.wait_ge(dsem[k], v)
                engobj[e](body)
        for e in self.ENG:
            if cum[e]:
                pool["cbase"][e] += cum[e][-1]
        pool["dcount"] = dict(self.dcount)


CF = ["ident", "ones", "tri0", "tri1", "ms0", "ms1", "mi0", "mi1", "bd64", "rp", "oz0", "oz1"]
LEVELS = [1, 2, 4, 8, 16, 32, 64]


def host_consts():
    i = np.arange(128)
    p, f = i[:, None], i[None, :]
    c = {}
    c["ident"] = (p == f)
    c["ones"] = np.ones((128, 128), bool)
    c["tri0"] = (p <= f)
    c["tri1"] = (p >= f)
    c["ms0"] = (f < p)
    c["ms1"] = (f > p)
    c["mi0"] = (f <= p)
    c["mi1"] = (f >= p)
    c["bd64"] = (p // 64 == f // 64)
    partner = np.where((i % 32) < 16, i + 16, i - 16)
    c["rp"] = (p == partner[None, :])
    c["oz0"] = np.broadcast_to(f < 64, (128, 128))
    c["oz1"] = np.broadcast_to(f >= 64, (128, 128))
    cf = np.stack([c[k].astype(np.float32) for k in CF], axis=1)
    lm = np.stack([((p // (2 * b) == f // (2 * b)) & (p // b != f // b)).astype(np.uint32) for b in LEVELS], axis=1)
    return np.ascontiguousarray(cf), np.ascontiguousarray(lm)


def rope_tables(flip):
    rows = T // 64
    freqs = (np.float32(10000.0) ** (-np.arange(16, dtype=np.float32) / np.float32(16))).astype(np.float32)
    tab = np.zeros((128, 4, 128), np.float32)
    for p in range(128):
        d = p % 64
        dd = d % 32
        sgn = -1.0 if dd < 16 else 1.0
        f = freqs[dd % 16]
        if d < 32:
            for m in range(rows):
                orow = (rows - 1 - m) if flip else m
                ang = np.float32(orow) * f
                tab[m, 0, p] = np.cos(ang)
                tab[m, 2, p] = np.sin(ang) * sgn
        else:
            for m in range(64):
                ocol = (63 - m) if flip else m
                ang = np.float32(ocol) * f
                tab[m, 1, p] = np.cos(ang)
                tab[m, 3, p] = np.sin(ang) * sgn
    return tab


def na_bias_tables(rpb, flip):
    def orig_rc(view_tile):
        u = view_tile * 128 + np.arange(128)
        t = (T - 1 - u) if flip else u
        return t // 64, t % 64

    def tile(qt, kt):
        qr, qc = orig_rc(qt)
        kr, kc = orig_rc(kt)
        r0 = np.clip(qr - 4, 0, T // 64 - 8)
        c0 = np.clip(qc - 8, 0, 48)
        valid = ((kr[:, None] >= r0[None, :]) & (kr[:, None] < r0[None, :] + 8)
                 & (kc[:, None] >= c0[None, :]) & (kc[:, None] < c0[None, :] + 16))
        dy = np.clip(kr[:, None] - qr[None, :] + 7, 0, 14)
        dx = np.clip(kc[:, None] - qc[None, :], -15, 15) + 15
        g = rpb[:, dy, dx]
        return np.where(valid[None], g, np.float32(NEG)).astype(np.float32)

    tabs = []
    for off in range(-2, 3):
        tabs.append(tile(2, 2 + off))
    for qt in range(2):
        for kt in range(4):
            tabs.append(tile(qt, kt))
    out = np.concatenate(tabs, axis=0)
    return np.ascontiguousarray(out.transpose(1, 0, 2))


def fm(v, n=None):
    return np.ascontiguousarray(v.reshape(-1, 128).T)


def prep_core(inp, b, half, cf, lm):
    flip = half == 1
    x = inp["x"][b]
    ctx = inp["ctx"][b]
    w_in = inp["w_in"][0]
    m = {}
    m["xv"] = np.ascontiguousarray(x[::-1]) if flip else x
    m["ctxv"] = np.ascontiguousarray(ctx[::-1]) if flip else ctx
    m["cT"] = np.ascontiguousarray(np.stack([fm(inp["c"][b]), fm(inp["c_ctx"])], axis=2))
    m["mod_w"] = inp["mod_w"][0]
    m["mod_bT"] = fm(inp["mod_b"][0])
    m["modb_gate"] = np.ascontiguousarray(np.broadcast_to(inp["mod_b"][0][None, 2048:3072], (128, 1024)))
    m["norm_wT"] = fm(inp["norm_w"][0])
    m["wna"] = np.ascontiguousarray(w_in[:, 0:1536])
    m["wdn"] = np.ascontiguousarray(w_in[:, 2048:3584])
    wb = w_in[:, 4096:4112].reshape(D, 2, 8)
    wa = w_in[:, 4112:4128].reshape(D, 2, 8)
    if flip:
        wb, wa = wb[:, ::-1], wa[:, ::-1]
    m["wba"] = np.ascontiguousarray(np.concatenate([wb.reshape(D, 16), wa.reshape(D, 16)], axis=1))
    m["wz"] = np.ascontiguousarray(np.concatenate([w_in[:, 1536:2048], w_in[:, 3584:4096], w_in[:, 4128:6176]], axis=1))
    cw = inp["conv_w"][0]
    cw = cw[::-1] if flip else cw
    m["conv_wT"] = np.ascontiguousarray(cw.T.reshape(12, 128, 3).transpose(1, 0, 2))
    m["qk_normT"] = np.ascontiguousarray(np.stack([np.tile(inp["na_q_norm"][0], 2), np.tile(inp["na_k_norm"][0], 2)], axis=1))
    m["dnnorm_b"] = np.ascontiguousarray(np.broadcast_to(inp["dn_norm_w"][0][None, :], (128, 64)))
    al, dtb = inp["dn_A_log"][0], inp["dn_dt_bias"][0]
    if flip:
        al, dtb = al[::-1], dtb[::-1]
    m["alog_b"] = np.ascontiguousarray(np.broadcast_to(al.reshape(1, 16), (128, 16)))
    m["dtb_b"] = np.ascontiguousarray(np.broadcast_to(dtb.reshape(1, 16), (128, 16)))
    m["ropet"] = rope_tables(flip)
    m["nabias"] = na_bias_tables(inp["na_rpb"][0], flip)
    m["w_o_na"] = inp["w_o_na"][0]
    m["w_o_dn"] = inp["w_o_dn"][0]
    m["w_out"] = inp["w_out"][0]
    m["cf"] = cf
    m["lm"] = lm
    return m


IN_SHAPES = {
    "xv": ([T, D], F32), "ctxv": ([CTX, D], F32), "cT": ([128, 8, 2], F32), "mod_w": ([D, 3072], F32),
    "mod_bT": ([128, 24], F32), "modb_gate": ([128, 1024], F32), "norm_wT": ([128, 8], F32),
    "wna": ([D, 1536], F32), "wdn": ([D, 1536], F32), "wba": ([D, 32], F32), "wz": ([D, 3072], F32),
    "conv_wT": ([128, 12, 3], F32), "qk_normT": ([128, 2], F32), "dnnorm_b": ([128, 64], F32),
    "alog_b": ([128, 16], F32), "dtb_b": ([128, 16], F32), "ropet": ([128, 4, 128], F32),
    "nabias": ([128, 104, 128], F32), "w_o_na": ([512, D], F32), "w_o_dn": ([512, D], F32), "w_out": ([D, D], F32),
    "cf": ([128, len(CF), 128], F32), "lm": ([128, len(LEVELS), 128], U32),
}


def b3(ap2, n):
    return ap2.unsqueeze(2).to_broadcast([ap2.shape[0], ap2.shape[1], n])


def bm(ap2, k):
    return ap2.unsqueeze(1).to_broadcast([ap2.shape[0], k, ap2.shape[1]])


class Ctx:
    pass


class StopBuild(Exception):
    pass


PHASE_IN = {
    0: ["cf", "lm", "cT", "mod_w", "mod_bT", "modb_gate", "norm_wT"],
    1: ["cf", "lm", "ctxv", "wdn", "wba", "conv_wT", "alog_b", "dtb_b", "dnnorm_b"],
    2: ["cf", "lm", "xv", "wdn", "wba", "conv_wT", "alog_b", "dtb_b", "dnnorm_b", "ropet"],
    3: ["cf", "lm", "xv", "wdn", "wba", "conv_wT", "alog_b", "dtb_b", "dnnorm_b", "ropet"],
    4: ["cf", "lm", "xv", "ctxv", "wna", "qk_normT", "nabias"],
    5: ["cf", "lm", "xv", "wz", "w_o_na", "w_o_dn", "w_out"],
}


def in_shapes(phases, dbg_in=()):
    sh = dict(IN_SHAPES)
    sh["xv"] = ([T, D], F32)
    names = []
    for ph in phases:
        for k in PHASE_IN[ph]:
            if k not in names:
                names.append(k)
    out = {k: sh[k] for k in names}
    for k in dbg_in:
        out[k] = DBG_IN[k] if DBG_IN[k] is not None else (([512, OWN], F32) if k == "ona_in" else ([OWN, 512], F32))
    return out


DBG_IN = {"ab_in": ([128, 4, 8], F32), "gate_in": ([128, 1024], F32), "Sx_in": ([128, 2, 512], F32),
          "ona_in": None, "odn_in": None}


def build(phases=(0, 1, 2, 3, 4, 5), dbg=None, dbg_in=(), stop=None):
    nc = bass.Bass("TRN2", target_bir_lowering=False)
    nblk = OWN // 512
    nblk_all = T // 512
    I = {k: nc.dram_tensor(k, s, d, kind="ExternalInput").ap() for k, (s, d) in in_shapes(phases, dbg_in).items()}
    out_d = nc.dram_tensor("out", [OWN, D], F32, kind="ExternalOutput").ap()
    dbg = dbg or {}
    skind = "ExternalOutput" if "scr" in dbg else "Internal"
    of_d = nc.dram_tensor("of_scr", [OWN, 512], F32, kind=skind).ap()
    odn_d = nc.dram_tensor("odn_scr", [OWN, 512], F32, kind=skind).ap()
    ona_d = nc.dram_tensor("ona_scr", [512, OWN], F32, kind=skind).ap()
    dbg = {k: v for k, v in dbg.items() if k != "scr"}
    dbg_d = {k: nc.dram_tensor("dbg_" + k, list(s), F32, kind="ExternalOutput").ap() for k, s in dbg.items()}

    with contextlib.ExitStack() as outer:
        def sbo(n, s, d=F32):
            return outer.enter_context(nc.sbuf_tensor("g_" + n, s, d))
        G = Ctx()
        G.cf = sbo("cf", [128, len(CF), 128])
        G.lm = sbo("lm", [128, len(LEVELS), 128], U32)
        G.c = {k: G.cf[:, i, :] for i, k in enumerate(CF)}
        G.ab = sbo("ab", [128, 4, 8])
        G.gate = sbo("gate_bc", [128, 1024])
        G.Sx = [sbo("Sx%d" % d, [128, 4, 2, 64]) for d in range(2)]
        G.cb = sbo("cb", [128, 8])
        G.pp = [outer.enter_context(nc.psum_tensor("pp%d" % i, [128, 1024], F32)) for i in range(2)]
        G.pb = [outer.enter_context(nc.psum_tensor("pb%d" % i, [128, 512], F32)) for i in range(4)]
        G.ppi = 0
        G.pbi = 0

        def pair():
            G.ppi += 1
            return G.pp[G.ppi % 2]

        def bank():
            G.pbi += 1
            return G.pb[G.pbi % 4]

        pool = {"stack": outer, "dsem": {}, "dcount": {}, "phase": 0}

        def run_phase(fn):
            with contextlib.ExitStack() as st:
                P = Prog(nc, pool)

                def sb(n, s, d=F32):
                    return st.enter_context(nc.sbuf_tensor("p%d_%s" % (pool["phase"], n), s, d))
                try:
                    fn(P, sb)
                except StopBuild:
                    pass
                P.emit()
            nc.all_engine_barrier()

        def chk(n):
            if stop is not None and n >= stop:
                raise StopBuild()

        def dump(P, name, ap):
            if name in dbg_d:
                P.dma(dbg_d[name], ap)

        def load_w(P, stg, dst, src, nk, ncols, scale_bc=None):
            srcv = src.rearrange("(kc p) n -> p kc n", p=128)
            for i, c0 in enumerate(range(0, ncols, 256)):
                w = min(256, ncols - c0)
                s = stg[i % 2]
                P.dma(s[:, 0:nk, 0:w], srcv[:, :, c0:c0 + w])
                E = "dve" if i % 2 == 0 else "pool"
                if scale_bc is None:
                    P.cp(E, dst[:, :, c0:c0 + w], s[:, 0:nk, 0:w])
                else:
                    P.tt(E, dst[:, :, c0:c0 + w], s[:, 0:nk, 0:w], bm(scale_bc[:, c0:c0 + w], nk), ALU.mult)

        def ht_block(P, B, src_d, tok0, NT, seqlen, mod, halo=True, keep_x=None):
            N = NT * 128
            aT, bT = G.ab[:, 2 * mod, :], G.ab[:, 2 * mod + 1, :]
            for t in range(NT):
                xt = keep_x[:, t, :] if keep_x is not None else B.x[t % 2][:]
                xn = B.xn[t % 2][:] if keep_x is not None else xt
                P.dma(xt, src_d[tok0 + t * 128: tok0 + (t + 1) * 128, :])
                P.actv(B.junk[:], xt, AF.Square, accum_out=B.ss[:, 0:1])
                P.actv(B.ss[:, 1:2], B.ss[:, 0:1], AF.Ln, scale=1.0 / D, bias=G.cb[:, 0:1])
                P.actv(B.ss[:, 2:3], B.ss[:, 1:2], AF.Exp, scale=-0.5)
                P.actv(xn, xt, AF.Identity, scale=B.ss[:, 2:3])
                pp = pair()
                for kc in range(8):
                    P.tr(pp[:, kc * 128:(kc + 1) * 128], xn[:, kc * 128:(kc + 1) * 128], G.c["ident"])
                for kc in range(8):
                    o = B.hT[:, kc, 1 + t * 128: 1 + (t + 1) * 128]
                    if kc % 2 == 0:
                        P.actv(o, pp[:, kc * 128:(kc + 1) * 128], AF.Identity, scale=aT[:, kc:kc + 1], bias=bT[:, kc:kc + 1])
                    else:
                        P.ts("dve", o, pp[:, kc * 128:(kc + 1) * 128], aT[:, kc:kc + 1], bT[:, kc:kc + 1], ALU.mult, ALU.add)
            if not halo:
                return
            left, right = tok0 - 1, tok0 + N
            lo, ro = max(left, 0), min(right, seqlen - 1)
            P.dma(B.xh[0:1, :], src_d[lo:lo + 1, :])
            P.dma(B.xh[1:2, :], src_d[ro:ro + 1, :])
            P.actv(B.junk[0:2, :], B.xh[:], AF.Square, accum_out=B.ssh[:, 0:1])
            P.actv(B.ssh[:, 1:2], B.ssh[:, 0:1], AF.Ln, scale=1.0 / D, bias=G.cb[0:2, 0:1])
            P.actv(B.ssh[:, 2:3], B.ssh[:, 1:2], AF.Exp, scale=-0.5)
            P.actv(B.xh[:], B.xh[:], AF.Identity, scale=B.ssh[:, 2:3])
            pb = bank()
            for kc in range(8):
                P.tr(pb[:, kc * 2:(kc + 1) * 2], B.xh[0:2, kc * 128:(kc + 1) * 128], G.c["ident"][0:2, 0:2])
            pv = pb[:, 0:16].rearrange("p (k t) -> p k t", t=2)
            P.tt("dve", B.hh[:], pv, b3(aT, 2), ALU.mult)
            hv = B.hT[:, :, 0:N + 2:N + 1]
            P.tt("dve", hv, B.hh[:], b3(bT, 2), ALU.add)
            if left < 0:
                P.memset("dve", B.hT[:, :, 0:1], 0.0)
            if right > seqlen - 1:
                P.memset("dve", B.hT[:, :, N + 1:N + 2], 0.0)

        def proj_fm(P, B, W, c0, N, hoff=1):
            pb = bank()
            for kc in range(8):
                P.mm(pb[:, 0:N], W[:, kc, c0:c0 + 128], B.hT[:, kc, hoff:hoff + N], start=(kc == 0), stop=(kc == 7))
            return pb

        def phase0(P, sb):
            P.dma(G.cf[:], I["cf"])
            P.dma(G.lm[:], I["lm"])
            P.memset("dve", G.cb[:, 0:1], EPS)
            P.memset("dve", G.cb[:, 1:2], 1.0)
            P.memset("dve", G.cb[:, 2:3], math.log(0.125))
            P.memset("dve", G.cb[:, 3:4], 0.0)
            for d in range(2):
                P.memset("pool", G.Sx[d][:], 0.0)
            cT = sb("cT", [128, 8, 2]); scT = sb("scT", [128, 8, 2]); scB = sb("scB", [128, 8, 128])
            mbT = sb("mbT", [128, 24]); mbg = sb("mbg", [128, 1024]); nwT = sb("nwT", [128, 8])
            modT = sb("modT", [128, 16, 2])
            stg = [sb("mstg%d" % i, [128, 8, 512]) for i in range(2)]
            P.dma(cT[:], I["cT"]); P.dma(mbT[:], I["mod_bT"]); P.dma(mbg[:], I["modb_gate"]); P.dma(nwT[:], I["norm_wT"])
            P.actv(scT[:], cT[:], AF.Silu)
            P.cp("dve", scB[:], b3(scT[:, :, 0], 128))
            mw = I["mod_w"].rearrange("(kc p) n -> p kc n", p=128)
            for q in range(6):
                s = stg[q % 2]
                P.dma(s[:], mw[:, :, q * 512:(q + 1) * 512])
                if q < 4:
                    for ch in range(4):
                        pb = bank()
                        for kc in range(8):
                            P.mm(pb[:, 0:2], s[:, kc, ch * 128:(ch + 1) * 128], scT[:, kc, :], start=(kc == 0), stop=(kc == 7))
                        cg = q * 4 + ch
                        P.ts("dve", modT[:, cg, :], pb[:, 0:2], mbT[:, cg:cg + 1], None, ALU.add)
                else:
                    pb = bank()
                    for kc in range(8):
                        P.mm(pb[:, 0:512], scB[:, kc, :], s[:, kc, :], start=(kc == 0), stop=(kc == 7))
                    P.tt("dve", G.gate[:, (q - 4) * 512:(q - 3) * 512], pb[:, 0:512], mbg[:, (q - 4) * 512:(q - 3) * 512], ALU.add)
            for m_ in range(2):
                P.ts("dve", G.ab[:, 2 * m_, :], modT[:, 8:16, m_], 1.0, None, ALU.add)
                P.tt("dve", G.ab[:, 2 * m_, :], G.ab[:, 2 * m_, :], nwT[:], ALU.mult)
                P.cp("dve", G.ab[:, 2 * m_ + 1, :], modT[:, 0:8, m_])
            dump(P, "ab", G.ab[:])
            dump(P, "gate", G.gate[:])

        def gdn_buffers(P, sb):
            B = Ctx()
            B.x = [sb("x%d" % i, [128, D]) for i in range(2)]
            B.junk = sb("junk", [128, D], BF16)
            B.ss = sb("ss", [128, 4]); B.xh = sb("xh", [2, D]); B.ssh = sb("ssh", [2, 4]); B.hh = sb("hh", [128, 8, 2])
            B.hT = sb("hT", [128, 8, 514], BF16)
            B.wdn = sb("wdn", [128, 8, 1536], BF16)
            B.wba = sb("wba", [128, 8, 32], BF16)
            B.convw = sb("convw", [128, 12, 3])
            B.negA = sb("negA", [128, 16]); B.dtb = sb("dtb", [128, 16])
            B.raw = [sb("raw%d" % i, [128, 514]) for i in range(2)]
            B.acc = [sb("acc%d" % i, [128, 512]) for i in range(2)]
            B.sq = B.acc[1]; B.lnt = sb("lnt", [128, 512]); B.rstd = B.lnt; B.tmp = B.acc[0]
            B.cos = sb("cos", [128, 512]); B.sin = sb("sin", [128, 512])
            B.qT = sb("qT", [128, 4, 512]); B.kT = sb("kT", [128, 4, 512]); B.vT = sb("vT", [128, 4, 512])
            B.stg = [B.qT[:].rearrange("p a (b c) -> p (a b) c", c=256), B.kT[:].rearrange("p a (b c) -> p (a b) c", c=256)]
            B.bc = sb("bc", [128, 4, 8]); B.gcol = sb("gcol", [128, 4, 8]); B.bg = sb("bgt", [128, 4, 8])
            B.vnew = sb("vnew", [128, 8, 64]); B.o1 = sb("o1", [128, 8, 64]); B.o = sb("o", [128, 8, 64])
            B.TB = []
            for j in range(2):
                TB = Ctx()
                TB.Kt = sb("Kt%d" % j, [128, 8, 64]); TB.Vt = sb("Vt%d" % j, [128, 8, 64]); TB.bV = TB.Vt
                TB.kd = TB.Kt; TB.Kbz = sb("Kbz%d" % j, [128, 8, 2, 64]); TB.U = sb("U%d" % j, [128, 8, 64])
                TB.gcs = sb("gcs%d" % j, [128, 16]); TB.sm = sb("tsm%d" % j, [128, 40]); TB.on = sb("ton%d" % j, [128, 16])
                TB.kbT = sb("kbT%d" % j, [128, 4, 128])
                TB.kz = [sb("kz%d_%d" % (j, i), [128, 4, 128]) for i in range(2)]
                TB.qz = [sb("qz%d_%d" % (j, i), [128, 4, 128]) for i in range(2)]
                TB.WT = sb("WT%d" % j, [128, 4, 128])
                TB.big = [sb("big%d_%d" % (j, i), [128, 8, 128], F32 if i == 3 else GDT) for i in range(7)]
                TB.t2 = sb("t2_%d" % j, [128, 8, 128])
                TB.t1 = TB.big[3]
                TB.rb = TB.t2[:, 0:4, :]
                B.TB.append(TB)
            B.of = sb("ofl", [128, 8, 64]); B.osq = B.o1
            B.sm = sb("sm", [128, 40]); B.on = sb("on", [128, 16])
            B.dnw = sb("dnw", [128, 64])
            B.ropet = sb("ropet", [128, 4, 128])
            if "ropet" in I:
                P.dma(B.ropet[:], I["ropet"])
            load_w(P, B.stg, B.wdn, I["wdn"], 8, 1536)
            load_w(P, B.stg, B.wba, I["wba"], 8, 32)
            P.dma(B.convw[:], I["conv_wT"]); P.dma(B.dtb[:], I["dtb_b"]); P.dma(B.negA[:], I["alog_b"]); P.dma(B.dnw[:], I["dnnorm_b"])
            P.actv(B.negA[:], B.negA[:], AF.Exp)
            P.ts("dve", B.negA[:], B.negA[:], -1.0, None, ALU.mult)
            for TB in B.TB:
                for i in range(2):
                    P.memset("pool", TB.kz[i][:], 0.0)
                    P.memset("pool", TB.qz[i][:], 0.0)
                P.memset("pool", TB.Kbz[:], 0.0)
            return B

        def gdn_stage1(P, B, src_d, tok0, NT, seqlen, mod, need_q, rope, dirs):
            N = NT * 128
            ht_block(P, B, src_d, tok0, NT, seqlen, mod)
            chk(1)
            if rope:
                r0 = tok0 // 64
                ohr = G.c["ident"][:, r0:r0 + 8].unsqueeze(2).to_broadcast([128, 8, 64])
                ohc = G.c["ident"][:, 0:64].unsqueeze(1).to_broadcast([128, 8, 64])
                for i_, dstt in enumerate((B.cos, B.sin)):
                    pb = bank()
                    pv_ = pb[:, 0:512].rearrange("p (r c) -> p r c", r=8)
                    P.mm(pv_, B.ropet[:, 2 * i_, :], ohr, start=True, stop=False)
                    P.mm(pv_, B.ropet[:, 2 * i_ + 1, :], ohc, start=False, stop=True)
                    P.cp("act", dstt[:, 0:512], pb[:, 0:512])
            groups = list(range(12)) if need_q else list(range(4, 12))
            dst = lambda gi: (B.qT, B.kT, B.vT)[gi // 4][:, gi % 4, 0:N]
            for n_, gi in enumerate(groups):
                pb = proj_fm(P, B, B.wdn, gi * 128, N)
                ph = bank()
                for kc in range(8):
                    P.mm(ph[:, 0:2], B.wdn[:, kc, gi * 128:(gi + 1) * 128], B.hT[:, kc, 0:N + 2:N + 1], start=(kc == 0), stop=(kc == 7))
                raw = B.raw[n_ % 2]
                acc = B.acc[n_ % 2]
                P.cp("act", raw[:, 1:N + 1], pb[:, 0:N])
                P.cp("act", raw[:, 0:N + 2:N + 1], ph[:, 0:2])
                E = "dve"
                P.ts(E, acc[:, 0:N], raw[:, 0:N], B.convw[:, gi, 0:1], None, ALU.mult)
                P.stt(acc[:, 0:N], raw[:, 1:N + 1], B.convw[:, gi, 1:2], acc[:, 0:N], ALU.mult, ALU.add)
                P.stt(acc[:, 0:N], raw[:, 2:N + 2], B.convw[:, gi, 2:3], acc[:, 0:N], ALU.mult, ALU.add)
                P.actv(dst(gi), acc[:, 0:N], AF.Silu)
            chk(2)
            for gi in groups:
                if gi >= 8:
                    continue
                s = dst(gi)
                P.tt("pool", B.sq[:, 0:N], s, s, ALU.mult)
                pb = bank()
                P.mm(pb[:, 0:N], G.c["bd64"], B.sq[:, 0:N])
                P.actv(B.lnt[:, 0:N], pb[:, 0:N], AF.Ln, bias=G.cb[:, 0:1])
                P.actv(B.rstd[:, 0:N], B.lnt[:, 0:N], AF.Exp, scale=-0.5, bias=(G.cb[:, 2:3] if gi < 4 else G.cb[:, 3:4]))
                P.tt("dve", s, s, B.rstd[:, 0:N], ALU.mult)
                if rope:
                    pr_ = bank()
                    P.mm(pr_[:, 0:N], G.c["rp"], s)
                    P.tt("dve", B.tmp[:, 0:N], pr_[:, 0:N], B.sin[:, 0:N], ALU.mult)
                    P.tt("pool", s, s, B.cos[:, 0:N], ALU.mult)
                    P.tt("dve", s, s, B.tmp[:, 0:N], ALU.add)
            chk(3)
            for t in range(NT):
                pb = bank()
                for kc in range(8):
                    P.mm(pb[:, 0:32], B.hT[:, kc, 1 + t * 128:1 + (t + 1) * 128], B.wba[:, kc, :], start=(kc == 0), stop=(kc == 7))
                for di, d in enumerate(dirs):
                    bco = B.bc[:, t, :] if di == 0 else B.bc2[:, t, :]
                    gco = B.gcol[:, t, :] if di == 0 else B.gcol2[:, t, :]
                    P.actv(B.sm[:, 0:8], pb[:, d * 8:(d + 1) * 8], AF.Exp, scale=-1.0)
                    P.ts("dve", B.sm[:, 8:16], B.sm[:, 0:8], 1.0, None, ALU.add)
                    P.recip(bco, B.sm[:, 8:16])
                    P.tt("dve", B.sm[:, 16:24], pb[:, 16 + d * 8:16 + (d + 1) * 8], B.dtb[:, d * 8:(d + 1) * 8], ALU.add)
                    P.actv(B.sm[:, 24:32], B.sm[:, 16:24], AF.Exp)
                    P.actv(B.sm[:, 32:40], B.sm[:, 24:32], AF.Ln, bias=G.cb[:, 1:2])
                    P.tt("dve", gco, B.sm[:, 32:40], B.negA[:, d * 8:(d + 1) * 8], ALU.mult)

        def gdn_tile_pre(P, B, TB, d, t, full, bc, gcol):
            chk(4)
            cs = slice(t * 128, (t + 1) * 128)
            c = G.c
            big = TB.big
            pb = bank()
            for pr in range(4):
                P.tr(pb[:, pr * 128:(pr + 1) * 128], B.kT[:, pr, cs], c["ident"])
            P.cp("act", TB.Kt[:].rearrange("p h d -> p (h d)"), pb[:, 0:512])
            pb = bank()
            for pr in range(4):
                P.tr(pb[:, pr * 128:(pr + 1) * 128], B.vT[:, pr, cs], c["ident"])
            P.cp("act", TB.Vt[:].rearrange("p h d -> p (h d)"), pb[:, 0:512])
            chk(5)
            yield
            pb = bank()
            P.mm(pb[:, 0:8], c["tri%d" % d], gcol)
            P.mm(pb[:, 8:16], c["ones"], gcol)
            P.cp("dve", TB.gcs[:], pb[:, 0:16])
            gc, gsum = TB.gcs[:, 0:8], TB.gcs[:, 8:16]
            egc, egl, gtot, beg, tmp8 = TB.on[:, 0:8], TB.sm[:, 0:8], TB.sm[:, 8:16], TB.sm[:, 16:24], TB.sm[:, 24:32]
            P.actv(egc, gc, AF.Exp)
            P.tt("dve", tmp8, gsum, gc, ALU.subtract)
            P.actv(egl, tmp8, AF.Exp)
            P.actv(gtot, gsum, AF.Exp)
            P.tt("dve", beg, bc, egc, ALU.mult)
            chk(6)
            yield
            for hh in range(2):
                P.tt("pool", TB.rb, bm(c["ident"], 4), b3(bc[:, hh::2], 128), ALU.mult)
                pb = bank()
                P.mm(pb[:, 0:512], c["ones"], TB.rb.rearrange("p a b -> p (a b)"))
                hs = slice(hh * 64, (hh + 1) * 64)
                P.tt("dve", TB.kbT[hs], B.kT[hs, :, cs], pb[hs, 0:512].rearrange("p (a b) -> p a b", a=4), ALU.mult)
                P.cp("pool", TB.kz[hh][hs], B.kT[hs, :, cs])
                if full:
                    P.cp("pool", TB.qz[hh][hs], B.qT[hs, :, cs])
            chk(7)
            yield
            t1, t2 = TB.t1, TB.t2
            P.tt("pool", t1[:], bm(c["tri%d" % d], 8), b3(gcol, 128), ALU.mult)
            pp = pair()
            P.mm(pp[:, 0:512], c["ones"], t1[:, 0:4, :].rearrange("p a b -> p (a b)"))
            P.mm(pp[:, 512:1024], c["ones"], t1[:, 4:8, :].rearrange("p a b -> p (a b)"))
            ppv = lambda x: x[:, 0:1024].rearrange("p (h f) -> p h f", h=8)
            P.tt("dve", t1[:], ppv(pp), b3(gc, 128), ALU.subtract)
            P.ts("dve", t2[:], t1[:], -1.0, 0.0, ALU.mult, ALU.min)
            P.actv(t2[:], t2[:], AF.Exp)
            P.tt("dve", big[1][:], t2[:], bm(c["ms%d" % d], 8), ALU.mult)
            P.ts("dve", t2[:], t1[:], 0.0, None, ALU.min)
            P.actv(t2[:], t2[:], AF.Exp)
            if full:
                P.tt("pool", big[3][:], t2[:], bm(c["mi%d" % (1 - d)], 8), ALU.mult)
            P.tt("dve", big[2][:], t2[:], bm(c["ms%d" % (1 - d)], 8), ALU.mult)
            chk(8)
            yield
            pp = pair()
            for h in range(8):
                P.mm(pp[:, h * 128:(h + 1) * 128], TB.kbT[:, h // 2, :], TB.kz[h % 2][:, h // 2, :])
            P.tt("dve", big[1][:], ppv(pp), big[1][:], ALU.mult)
            yield
            pp = pair()
            for h in range(8):
                P.mm(pp[:, h * 128:(h + 1) * 128], TB.kz[h % 2][:, h // 2, :], TB.kbT[:, h // 2, :])
            P.tt("dve", big[2][:], ppv(pp), big[2][:], ALU.mult)
            yield
            if full:
                pp = pair()
                for h in range(8):
                    P.mm(pp[:, h * 128:(h + 1) * 128], B.kT[:, h // 2, cs], TB.qz[h % 2][:, h // 2, :])
                P.tt("dve", big[3][:], ppv(pp), big[3][:], ALU.mult)
            L_, N_, QK_, X_, D_, Fn, F2n = big[1], big[2], big[3], big[4], big[5], big[0], big[6]
            chk(9)
            yield
            lmf = B.lm1f
            P.tt("dve", X_[:], L_[:], bm(lmf[:], 8), ALU.mult)
            P.tt("dve", X_[:], bm(c["ident"], 8), X_[:], ALU.subtract)
            P.tt("pool", D_[:], N_[:], bm(lmf[:], 8), ALU.mult)
            P.tt("pool", D_[:], bm(c["ident"], 8), D_[:], ALU.subtract)
            chk(10)
            yield
            for li in range(1, len(LEVELS)):
                last = li == len(LEVELS) - 1
                mask = bm(G.lm[:, li, :], 8)
                ppF = pair()
                for h in range(8):
                    P.mm(ppF[:, h * 128:(h + 1) * 128], L_[:, h, :], D_[:, h, :])
                P.actv(Fn[:], ppv(ppF), AF.Identity, scale=-1.0)
                yield
                if not last:
                    ppF2 = pair()
                    for h in range(8):
                        P.mm(ppF2[:, h * 128:(h + 1) * 128], N_[:, h, :], X_[:, h, :])
                    P.actv(F2n[:], ppv(ppF2), AF.Identity, scale=-1.0)
                    yield
                ppG = pair()
                for h in range(8):
                    P.mm(ppG[:, h * 128:(h + 1) * 128], X_[:, h, :], Fn[:, h, :])
                if not last:
                    P.cp("act", TB.t2[:], ppv(ppG))
                    yield
                    ppG2 = pair()
                    for h in range(8):
                        P.mm(ppG2[:, h * 128:(h + 1) * 128], D_[:, h, :], F2n[:, h, :])
                    P.cpred(X_[:], mask, ppv(ppG2))
                    P.cpred(D_[:], mask, TB.t2[:])
                else:
                    P.cpred(D_[:], mask, ppv(ppG))
                yield
            chk(11)
            yield
            P.tt("pool", TB.bV[:], TB.Vt[:], b3(bc, 64), ALU.mult)
            for hh in range(2):
                P.tt("pool", TB.Kbz[:, hh::2, hh, :], TB.Kt[:, hh::2, :], b3(beg[:, hh::2], 64), ALU.mult)
            P.tt("pool", TB.kd[:], TB.Kt[:], b3(egl, 64), ALU.mult)
            if GDT != F32:
                P.cp("act", TB.t2[:], D_[:])
                D_ = TB.t2
            pb = bank()
            for h in range(8):
                P.mm(pb[:, h * 64:(h + 1) * 64], D_[:, h, :], TB.bV[:, h, :])
            P.cp("act", TB.U[:].rearrange("p h d -> p (h d)"), pb[:, 0:512])
            pb = bank()
            for pr in range(4):
                for hh in range(2):
                    h = 2 * pr + hh
                    P.mm(pb[:, pr * 128:(pr + 1) * 128], TB.Kbz[:, h, :, :].rearrange("p a b -> p (a b)"), D_[:, h, :],
                         start=(hh == 0), stop=(hh == 1))
            P.cp("act", TB.WT[:].rearrange("p a b -> p (a b)"), pb[:, 0:512])
        def gdn_tile_step(P, B, TB, d, t, full, out_cb):
            cs = slice(t * 128, (t + 1) * 128)
            Sx = G.Sx[d]
            QK_ = TB.big[3]
            egc, gtot = TB.on[:, 0:8], TB.sm[:, 8:16]
            pb = bank()
            for pr in range(4):
                P.mm(pb[:, pr * 128:(pr + 1) * 128], TB.WT[:, pr, :], Sx[:, pr, :, :].rearrange("p a b -> p (a b)"))
            P.tt("dve", B.vnew[:].rearrange("p h d -> p (h d)"), TB.U[:].rearrange("p h d -> p (h d)"), pb[:, 0:512], ALU.subtract)
            if full:
                pb1 = bank()
                for pr in range(4):
                    P.mm(pb1[:, pr * 128:(pr + 1) * 128], B.qT[:, pr, cs], Sx[:, pr, :, :].rearrange("p a b -> p (a b)"))
                P.tt("dve", B.o1[:], pb1[:, 0:512].rearrange("p (h d) -> p h d", h=8), b3(egc, 64), ALU.mult)
                pb2 = bank()
                for h in range(8):
                    P.mm(pb2[:, h * 64:(h + 1) * 64], QK_[:, h, :], B.vnew[:, h, :])
                P.tt("dve", B.o[:].rearrange("p h d -> p (h d)"), B.o1[:].rearrange("p h d -> p (h d)"), pb2[:, 0:512], ALU.add)
            pbc = bank()
            for pr in range(4):
                P.mm(pbc[:, pr * 128:(pr + 1) * 128], TB.kd[:, 2 * pr:2 * pr + 2, :].rearrange("p a b -> p (a b)"),
                     B.vnew[:, 2 * pr:2 * pr + 2, :].rearrange("p a b -> p (a b)"))
            for hh in range(2):
                hs = slice(hh * 64, (hh + 1) * 64)
                sv = Sx[hs, :, hh, :]
                P.tt("pool", sv, sv, b3(gtot[hs, hh::2], 64), ALU.mult)
                cv = pbc[hs, 0:512].rearrange("p (a b d) -> p a b d", a=4, b=2)[:, :, hh, :]
                P.tt("dve", sv, sv, cv, ALU.add)
            if full:
                out_cb(t)


        def run_tiles(P, B, d, order, full, bcf, gcf, out_cb):
            for i in range(0, len(order), 2):
                grp = order[i:i + 2]
                gens = [gdn_tile_pre(P, B, B.TB[j], d, t, full, bcf(t), gcf(t)) for j, t in enumerate(grp)]
                alive = list(gens)
                while alive:
                    for g in list(alive):
                        try:
                            next(g)
                        except StopIteration:
                            alive.remove(g)
                for j, t in enumerate(grp):
                    gdn_tile_step(P, B, B.TB[j], d, t, full, out_cb)

        def gdn_phase_common(P, sb):
            B = gdn_buffers(P, sb)
            B.bc2 = sb("bc2", [128, 4, 8]); B.gcol2 = sb("gcol2", [128, 4, 8])
            B.lm1f = sb("lm1f", [128, 128])
            P.cp("dve", B.lm1f[:], G.lm[:, 0, :])
            return B

        def phase1(P, sb):
            B = gdn_phase_common(P, sb)
            gdn_stage1(P, B, I["ctxv"], 0, 2, CTX, 1, False, False, [0, 1])
            run_tiles(P, B, 0, [0, 1], False, lambda t: B.bc[:, t, :], lambda t: B.gcol[:, t, :], None)
            run_tiles(P, B, 1, [1, 0], False, lambda t: B.bc2[:, t, :], lambda t: B.gcol2[:, t, :], None)
            dump(P, "Sx0", G.Sx[0][:].rearrange("p a b d -> p (a b d)"))
            dump(P, "Sx1", G.Sx[1][:].rearrange("p a b d -> p (a b d)"))

        def phase2(P, sb):
            B = gdn_phase_common(P, sb)
            for ib in range(nblk):
                gdn_stage1(P, B, I["xv"], ib * 512, 4, T, 0, True, True, [0])

                def ocb(t, ib=ib):
                    r0 = ib * 512 + t * 128
                    P.dma(of_d[r0:r0 + 128, :], B.o[:].rearrange("p h d -> p (h d)"))
                run_tiles(P, B, 0, [0, 1, 2, 3], True, lambda t: B.bc[:, t, :], lambda t: B.gcol[:, t, :], ocb)
                if ib == 0:
                    dump(P, "kT0", B.kT[:].rearrange("p a b -> p (a b)"))
                    dump(P, "qT0", B.qT[:].rearrange("p a b -> p (a b)"))
                    dump(P, "vT0", B.vT[:].rearrange("p a b -> p (a b)"))
                    dump(P, "bc0", B.bc[:].rearrange("p a b -> p (a b)"))
                    dump(P, "gcol0", B.gcol[:].rearrange("p a b -> p (a b)"))

        def phase3(P, sb):
            B = gdn_phase_common(P, sb)
            if True:
                for ib in range(nblk_all - 1, nblk - 1, -1):
                    gdn_stage1(P, B, I["xv"], ib * 512, 4, T, 0, False, True, [1])
                    run_tiles(P, B, 1, [3, 2, 1, 0], False, lambda t: B.bc[:, t, :], lambda t: B.gcol[:, t, :], None)
            for ib in range(nblk - 1, -1, -1):
                gdn_stage1(P, B, I["xv"], ib * 512, 4, T, 0, True, True, [1])

                def ocb(t, ib=ib):
                    r0 = ib * 512 + t * 128
                    P.dma(B.of[:].rearrange("p h d -> p (h d)"), of_d[r0:r0 + 128, :])
                    P.tt("pool", B.o[:], B.o[:], B.of[:], ALU.add)
                    P.tt("pool", B.osq[:], B.o[:], B.o[:], ALU.mult)
                    P.reduce("dve", B.on[:, 8:16], B.osq[:], ALU.add)
                    P.actv(B.sm[:, 32:40], B.on[:, 8:16], AF.Ln, scale=1.0 / 64, bias=G.cb[:, 0:1])
                    P.actv(B.sm[:, 32:40], B.sm[:, 32:40], AF.Exp, scale=-0.5)
                    P.tt("dve", B.osq[:], B.o[:], b3(B.sm[:, 32:40], 64), ALU.mult)
                    P.tt("dve", B.osq[:], B.osq[:], bm(B.dnw[:], 8), ALU.mult)
                    P.dma(odn_d[r0:r0 + 128, :], B.osq[:].rearrange("p h d -> p (h d)"))
                run_tiles(P, B, 1, [3, 2, 1, 0], True, lambda t: B.bc[:, t, :], lambda t: B.gcol[:, t, :], ocb)

        def phase4(P, sb):
            B = Ctx()
            B.x = [sb("x%d" % i, [128, D]) for i in range(2)]
            B.junk = sb("junk", [128, D], BF16)
            B.ss = sb("ss", [128, 4])
            B.hT = sb("hT", [128, 8, 514], BF16)
            B.wna = sb("wna", [128, 8, 1536], BF16)
            B.stg = [sb("stg%d" % i, [128, 8, 256]) for i in range(2)]
            bias = sb("nabias", [128, 104, 128], BF16)
            identb = sb("identb", [128, 128], BF16); bd64b = sb("bd64b", [128, 128], BF16)
            ozb = [sb("ozb%d" % i, [128, 128], BF16) for i in range(2)]
            wq = sb("wqk", [128, 2])
            qz = [[sb("qz%d_%d" % (s, hh), [128, 4, 512], BF16) for hh in range(2)] for s in range(2)]
            kTr = [sb("kTr%d" % s, [128, 4, 512], BF16) for s in range(3)]
            Vz = [sb("Vz%d" % s, [128, 4, 8, 2, 64], BF16) for s in range(3)]
            kcT = sb("kcT", [128, 4, 256], BF16)
            Vzc = sb("Vzc", [128, 2, 8, 2, 64], BF16)
            sq = sb("nsq", [128, 512], BF16); lnt = sb("nlnt", [128, 512]); rstd = sb("nrstd", [128, 512])
            Eb = [sb("E%d" % i, [128, 896], BF16) for i in range(3)]
            rden = sb("rden", [128, 128])
            oT = [sb("oT%d" % i, [128, 4, 512]) for i in range(2)]
            load_w(P, B.stg, B.wna, I["wna"], 8, 1536)
            nbv = I["nabias"]
            for i, t0 in enumerate(range(0, 104, 16)):
                n = min(16, 104 - t0)
                s = B.stg[i % 2]
                sv = s[:].rearrange("p a b -> p (a b)")[:, 0:n * 128].rearrange("p (a b) -> p a b", b=128)
                P.dma(sv, nbv[:, t0:t0 + n, :])
                P.cp("dve" if i % 2 == 0 else "pool", bias[:, t0:t0 + n, :], sv)
            P.cp("dve", identb[:], G.c["ident"])
            P.ts("dve", bd64b[:], G.c["bd64"], 1.0 / 64, None, ALU.mult)
            for i in range(2):
                P.cp("dve", ozb[i][:], G.c["oz%d" % i])
            P.dma(wq[:], I["qk_normT"])
            P.ts("dve", wq[:, 0:1], wq[:, 0:1], 0.125, None, ALU.mult)
            for s in range(2):
                for hh in range(2):
                    P.memset("pool", qz[s][hh][:], 0.0)
            for s in range(3):
                P.memset("pool", Vz[s][:], 0.0)
            P.memset("pool", Vzc[:], 0.0)

            def qknorm(pb, N, which, outs):
                P.actv(sq[:, 0:N], pb[:, 0:N], AF.Square)
                p2 = bank()
                P.mm(p2[:, 0:N], bd64b[:], sq[:, 0:N])
                P.actv(lnt[:, 0:N], p2[:, 0:N], AF.Ln, bias=G.cb[:, 0:1])
                P.actv(rstd[:, 0:N], lnt[:, 0:N], AF.Exp, scale=-0.5)
                for (o, ps_) in outs:
                    P.stt(o, pb[ps_, 0:N], wq[ps_, which:which + 1], rstd[ps_, 0:N], ALU.mult, ALU.mult)

            def v_tiles(NT, dstfn):
                for t in range(NT):
                    pb = bank()
                    for kc in range(8):
                        P.mm(pb[:, 0:512], B.hT[:, kc, 1 + t * 128:1 + (t + 1) * 128], B.wna[:, kc, 1024:1536], start=(kc == 0), stop=(kc == 7))
                    pv = pb[:, 0:512].rearrange("p (h d) -> p h d", d=64)
                    for hh in range(2):
                        P.cp("act" if hh == 0 else "dve", dstfn(t)[:, hh::2, hh, :], pv[:, hh::2, :])

            ht_block(P, B, I["ctxv"], 0, 2, CTX, 1, halo=False)
            for g in range(4):
                pb = proj_fm(P, B, B.wna, 512 + g * 128, 256)
                qknorm(pb, 256, 1, [(kcT[:, g, :], slice(0, 128))])
            v_tiles(2, lambda t: Vzc[:, t])

            full = slice(0, 128)

            def attend(ib):
                ob = oT[ib % 2]
                for j in range(4):
                    qt = ib * 4 + j
                    if qt < 2:
                        kts = [0, 1, 2, 3]
                        tb = lambda n_, h: 40 + (qt * 4 + n_) * 8 + h
                    else:
                        kts = [qt - 2 + o for o in range(5)]
                        tb = lambda n_, h: n_ * 8 + h
                    ncol = (len(kts) + 2) * 128
                    for pr in range(4):
                        pps = []
                        for hh in range(2):
                            h = 2 * pr + hh
                            pp = pair()
                            q_ = qz[ib % 2][hh][:, pr, j * 128:(j + 1) * 128]
                            for n_, kt in enumerate(kts):
                                kb, lt = kt // 4, kt % 4
                                P.mm(pp[:, n_ * 128:(n_ + 1) * 128], kTr[kb % 3][:, pr, lt * 128:(lt + 1) * 128], q_, start=True, stop=False)
                                P.mm(pp[:, n_ * 128:(n_ + 1) * 128], identb[:], bias[:, tb(n_, h), :], start=False, stop=True)
                            for ct in range(2):
                                c0 = (len(kts) + ct) * 128
                                P.mm(pp[:, c0:c0 + 128], kcT[:, pr, ct * 128:(ct + 1) * 128], q_)
                            pps.append(pp)
                        Es = []
                        for hh in range(2):
                            G.ei = getattr(G, "ei", 0) + 1
                            E = Eb[G.ei % 3]
                            P.actv(E[:, 0:ncol], pps[hh][:, 0:ncol], AF.Exp)
                            Es.append(E)
                        pn, pd = bank(), bank()
                        nmm = 2 * (len(kts) + 2)
                        i_ = 0
                        for hh in range(2):
                            h = 2 * pr + hh
                            for n_ in range(len(kts) + 2):
                                if n_ < len(kts):
                                    kt = kts[n_]
                                    vv = Vz[(kt // 4) % 3][:, kt % 4, h, :, :]
                                else:
                                    vv = Vzc[:, n_ - len(kts), h, :, :]
                                e_ = Es[hh][:, n_ * 128:(n_ + 1) * 128]
                                P.mm(pn[:, 0:128], vv.rearrange("p a b -> p (a b)"), e_, start=(i_ == 0), stop=(i_ == nmm - 1))
                                P.mm(pd[:, 0:128], ozb[hh][:], e_, start=(i_ == 0), stop=(i_ == nmm - 1))
                                i_ += 1
                        P.recip(rden[:], pd[:, 0:128])
                        P.tt("dve", ob[:, pr, j * 128:(j + 1) * 128], pn[:, 0:128], rden[:], ALU.mult)
                onav = ona_d.rearrange("(pr p) t -> p pr t", p=128)
                P.dma(onav[:, :, ib * 512:(ib + 1) * 512], ob[:])
                if ib == 0:
                    dump(P, "onaT0", ob[:].rearrange("p a b -> p (a b)"))

            for ib in range(nblk + 1):
                ht_block(P, B, I["xv"], ib * 512, 4, T, 0, halo=False)
                if ib < nblk:
                    for g in range(4):
                        pb = proj_fm(P, B, B.wna, g * 128, 512)
                        qknorm(pb, 512, 0, [(qz[ib % 2][0][0:64, g, :], slice(0, 64)), (qz[ib % 2][1][64:128, g, :], slice(64, 128))])
                for g in range(4):
                    pb = proj_fm(P, B, B.wna, 512 + g * 128, 512)
                    qknorm(pb, 512, 1, [(kTr[ib % 3][:, g, :], full)])
                v_tiles(4, lambda t, ib=ib: Vz[ib % 3][:, t])
                if ib >= 1:
                    attend(ib - 1)

        def phase5(P, sb):
            B = Ctx()
            B.xn = [sb("xn%d" % i, [128, D]) for i in range(2)]
            xk = sb("xk", [128, 4, D])
            B.junk = sb("junk", [128, D], BF16)
            B.ss = sb("ss", [128, 4])
            B.hT = sb("hT", [128, 8, 514], BF16)
            B.stg = [sb("stg%d" % i, [128, 8, 256]) for i in range(2)]
            wz = sb("wz", [128, 8, 3072], BF16)
            wona = sb("wona", [128, 4, 1024], BF16); wodn = sb("wodn", [128, 4, 1024], BF16)
            wout = sb("wout", [128, 8, 1024], BF16)
            onaT = sb("onaT", [128, 4, 512]); odn = sb("odn", [128, 4, 512])
            AnaT = sb("AnaT", [128, 4, 512], BF16); AdnT = sb("AdnT", [128, 4, 512], BF16)
            yT = sb("yT", [128, 8, 512], BF16)
            sz = [sb("sz%d" % i, [128, 512]) for i in range(2)]
            s1 = sb("s1", [128, 512]); s2 = sb("s2", [128, 512]); t1 = sb("mt1", [128, 512]); t2 = sb("mt2", [128, 512])
            ot = [sb("ot%d" % i, [128, D]) for i in range(2)]
            load_w(P, B.stg, wz, I["wz"], 8, 3072)
            load_w(P, B.stg, wona, I["w_o_na"], 4, 1024)
            load_w(P, B.stg, wodn, I["w_o_dn"], 4, 1024)
            load_w(P, B.stg, wout, I["w_out"], 8, 1024, scale_bc=G.gate)
            onav = I.get("ona_in", ona_d).rearrange("(pr p) t -> p pr t", p=128)
            odn_src = I.get("odn_in", odn_d)
            for ib in range(nblk):
                ht_block(P, B, I["xv"], ib * 512, 4, T, 0, halo=False, keep_x=xk)
                P.dma(onaT[:], onav[:, :, ib * 512:(ib + 1) * 512])
                P.dma(odn[:], odn_src[ib * 512:(ib + 1) * 512, :].rearrange("(t p) f -> p t f", p=128))
                for g in range(4):
                    pb = proj_fm(P, B, wz, g * 128, 512)
                    P.actv(sz[g % 2][:], pb[:, 0:512], AF.Silu)
                    P.tt("pool", AnaT[:, g, :], onaT[:, g, :], sz[g % 2][:], ALU.mult)
                for g in range(4):
                    pb = proj_fm(P, B, wz, 512 + g * 128, 512)
                    P.actv(sz[g % 2][:], pb[:, 0:512], AF.Silu)
                    pt = bank()
                    for t in range(4):
                        P.tr(pt[:, t * 128:(t + 1) * 128], odn[:, t, g * 128:(g + 1) * 128], G.c["ident"])
                    P.tt("dve", AdnT[:, g, :], pt[:, 0:512], sz[g % 2][:], ALU.mult)
                for j in range(8):
                    pg1 = proj_fm(P, B, wz, 1024 + j * 128, 512)
                    P.actv(s1[:], pg1[:, 0:512], AF.Sigmoid)
                    pg2 = proj_fm(P, B, wz, 2048 + j * 128, 512)
                    P.actv(s2[:], pg2[:, 0:512], AF.Sigmoid)
                    pu1 = bank()
                    for fc in range(4):
                        P.mm(pu1[:, 0:512], wona[:, fc, j * 128:(j + 1) * 128], AnaT[:, fc, :], start=(fc == 0), stop=(fc == 3))
                    P.tt("dve", t1[:], pu1[:, 0:512], s1[:], ALU.mult)
                    pu2 = bank()
                    for fc in range(4):
                        P.mm(pu2[:, 0:512], wodn[:, fc, j * 128:(j + 1) * 128], AdnT[:, fc, :], start=(fc == 0), stop=(fc == 3))
                    P.tt("dve", t2[:], pu2[:, 0:512], s2[:], ALU.mult)
                    P.tt("pool", yT[:, j, :], t1[:], t2[:], ALU.add)
                for t in range(4):
                    o_ = ot[t % 2]
                    for hf in range(2):
                        pb = bank()
                        for kc in range(8):
                            P.mm(pb[:, 0:512], yT[:, kc, t * 128:(t + 1) * 128], wout[:, kc, hf * 512:(hf + 1) * 512], start=(kc == 0), stop=(kc == 7))
                        P.tt("dve", o_[:, hf * 512:(hf + 1) * 512], pb[:, 0:512], xk[:, t, hf * 512:(hf + 1) * 512], ALU.add)
                    r0 = ib * 512 + t * 128
                    P.dma(out_d[r0:r0 + 128, :], o_[:])

        def phase_dbgin(P, sb):
            P.dma(G.cf[:], I["cf"])
            P.dma(G.lm[:], I["lm"])
            P.memset("dve", G.cb[:, 0:1], EPS)
            P.memset("dve", G.cb[:, 1:2], 1.0)
            P.memset("dve", G.cb[:, 2:3], math.log(0.125))
            P.memset("dve", G.cb[:, 3:4], 0.0)
            for d in range(2):
                P.memset("pool", G.Sx[d][:], 0.0)
            if "ab_in" in I:
                P.dma(G.ab[:], I["ab_in"])
            if "gate_in" in I:
                P.dma(G.gate[:], I["gate_in"])
            if "Sx_in" in I:
                for d in range(2):
                    P.dma(G.Sx[d][:].rearrange("p a b d -> p (a b d)"), I["Sx_in"][:, d, :])

        fns = {0: phase0, 1: phase1, 2: phase2, 3: phase3, 4: phase4, 5: phase5}
        if 0 not in phases:
            run_phase(phase_dbgin)
        for ph in phases:
            run_phase(fns[ph])
    return nc


_CACHE = {}


def kernel(**inputs):
    inp = {k: np.asarray(v) for k, v in inputs.items()}
    if "nc" not in _CACHE:
        _CACHE["nc"] = build()
    nc = _CACHE["nc"]
    cf, lm = host_consts()
    in_maps = []
    for core in range(8):
        b, half = core // 2, core % 2
        m = prep_core(inp, b, half, cf, lm)
        in_maps.append({k: np.ascontiguousarray(m[k]) for k in IN_SHAPES})
    res = run_bass_kernel_spmd(nc, in_maps, core_ids=list(range(8)))
    out = np.empty((4, T, D), np.float32)
    for core in range(8):
        b, half = core // 2, core % 2
        o = res.results[core]["out"]
        if half == 0:
            out[b, 0:OWN] = o
        else:
            out[b, OWN:T] = o[::-1]
    return out
```
